# Optimizing a Trainium2 kernel written in Bass

```python
import math
import jax, jax.numpy as jnp
from jax import lax
import numpy as np

D_MODEL = 2048
BATCH = 4
SEQ = 4096
DEPTH = 1
DEC_BATCH = 1
DEC_SEQ = 8192
PAST_LEN = 128

HEAD_DIM = 64
N_HEADS_A = 16
N_HEADS_B = 16
N_KV_B = 4
WIDTH_A = N_HEADS_A * HEAD_DIM
WIDTH_B = N_HEADS_B * HEAD_DIM
MIX_WIDTH = WIDTH_A + WIDTH_B
KV_WIDTH_B = N_KV_B * HEAD_DIM
IN_WIDTH = 3 * WIDTH_A + WIDTH_B + 2 * KV_WIDTH_B
DILATED_PATTERNS = ((128, 1), (512, 4), (2048, 16))
LOCAL_WINDOW = 128
PEER_HEADS = 8
PEER_NKEYS = 128
PEER_DKEY = 256
PEER_TOPK = 16
N_EXPERTS = PEER_NKEYS * PEER_NKEYS
TOKEN_CHUNK = 128
RMS_EPS = 1e-6
NEG = -1e30

kernel_name = "hymba_dilated_swa_peer_encoder"


def rmsnorm(x, g):
    xf = x.astype(jnp.float32)
    y = xf * lax.rsqrt(jnp.mean(xf * xf, axis=-1, keepdims=True) + RMS_EPS)
    return (y * g.astype(jnp.float32)).astype(x.dtype)


def alibi_slopes():
    n = N_HEADS_A + N_HEADS_B
    s = 2.0 ** (-8.0 * jnp.arange(1, n + 1, dtype=jnp.float32) / n)
    return s[0::2], s[1::2]


def banded_attn(q, k, v, slopes, half, step, sink=None):
    B, L, Hq, dh = q.shape
    Hk = k.shape[2]
    G = Hq // Hk
    W = half
    nb = -(-L // W)
    Lp = nb * W
    pad = Lp - L
    q = jnp.pad(q, ((0, 0), (0, pad), (0, 0), (0, 0)))
    kp = jnp.pad(k, ((0, 0), (W, pad + W), (0, 0), (0, 0))).reshape(B, nb + 2, W, Hk, dh)
    vp = jnp.pad(v, ((0, 0), (W, pad + W), (0, 0), (0, 0))).reshape(B, nb + 2, W, Hk, dh)
    kwin = jnp.concatenate([kp[:, :-2], kp[:, 1:-1], kp[:, 2:]], axis=2)
    vwin = jnp.concatenate([vp[:, :-2], vp[:, 1:-1], vp[:, 2:]], axis=2)
    qb = q.reshape(B, nb, W, Hk, G, dh)
    s = jnp.einsum('bnqkgd,bnskd->bnkgqs', qb, kwin).astype(jnp.float32) / math.sqrt(dh)
    rel = jnp.arange(3 * W)[None, :] - W - jnp.arange(W)[:, None]
    kpos = jnp.arange(nb)[:, None] * W + jnp.arange(3 * W)[None, :] - W
    valid = (jnp.abs(rel) <= W)[None] & ((kpos >= 0) & (kpos < L))[:, None, :]
    dist = (jnp.abs(rel) * step).astype(jnp.float32)
    bias = -slopes.astype(jnp.float32).reshape(Hk, G)[:, :, None, None] * dist
    s = jnp.where(valid[None, :, None, None], s + bias[None, None], NEG)
    m = jnp.max(s, axis=-1)
    if sink is not None:
        sk = sink.astype(jnp.float32).reshape(Hk, G)[:, :, None]
        m = jnp.maximum(m, sk)
    e = jnp.exp(s - m[..., None])
    den = jnp.sum(e, axis=-1)
    if sink is not None:
        den = den + jnp.exp(sk - m)
    lse = m + jnp.log(den)
    p = e / den[..., None]
    o = jnp.einsum('bnkgqs,bnskd->bnqkgd', p.astype(vwin.dtype), vwin)
    o = o.reshape(B, Lp, Hq, dh)[:, :L]
    lse = lse.transpose(0, 1, 4, 2, 3).reshape(B, Lp, Hq)[:, :L]
    return o, lse


def dilated_attention(q, k, v, slopes):
    B, S, H, dh = q.shape
    outs, lses = [], []
    for window, d in DILATED_PATTERNS:
        half = (window // 2) // d
        def split(t):
            return t.reshape(B, S // d, d, H, dh).transpose(0, 2, 1, 3, 4).reshape(B * d, S // d, H, dh)
        o, lse = banded_attn(split(q), split(k), split(v), slopes, half, d)
        outs.append(o.reshape(B, d, S // d, H, dh).transpose(0, 2, 1, 3, 4).reshape(B, S, H, dh).astype(jnp.float32))
        lses.append(lse.reshape(B, d, S // d, H).transpose(0, 2, 1, 3).reshape(B, S, H))
    w = jax.nn.softmax(jnp.stack(lses), axis=0)
    out = jnp.einsum('pbsh,pbshd->bshd', w, jnp.stack(outs))
    return out.astype(q.dtype)


def peer(h, w_pq, sub_keys, expert_u, expert_v):
    B, S, D = h.shape
    T = B * S
    hc = h.reshape(T // TOKEN_CHUNK, TOKEN_CHUNK, D)
    K = PEER_TOPK

    def chunk(xc):
        C = xc.shape[0]
        q = (xc @ w_pq).reshape(C, PEER_HEADS, 2, PEER_DKEY // 2)
        sc = jnp.einsum('chpd,hpnd->chpn', q, sub_keys).astype(jnp.float32)
        v1, i1 = lax.top_k(sc[:, :, 0], K)
        v2, i2 = lax.top_k(sc[:, :, 1], K)
        cand = (v1[..., :, None] + v2[..., None, :]).reshape(C, PEER_HEADS, K * K)
        cid = (i1[..., :, None] * PEER_NKEYS + i2[..., None, :]).reshape(C, PEER_HEADS, K * K)
        top, pos = lax.top_k(cand, K)
        eid = jnp.take_along_axis(cid, pos, axis=-1)
        g = jax.nn.softmax(top, axis=-1)
        ue = expert_u[eid]
        act = jax.nn.gelu(jnp.einsum('cd,chkd->chk', xc, ue).astype(jnp.float32), approximate=False)
        ve = expert_v[eid]
        return jnp.einsum('chk,chkd->cd', (g * act).astype(xc.dtype), ve)

    return lax.map(chunk, hc).reshape(B, S, D)


def trunk(x, norm1_g, w_in, a_out_g, b_out_g, sink, w_out, norm2_g, w_pq, sub_keys, expert_u, expert_v, normf_g):
    B, S, _ = x.shape
    slopes_a, slopes_b = alibi_slopes()
    cuts = [WIDTH_A, 2 * WIDTH_A, 3 * WIDTH_A, 3 * WIDTH_A + WIDTH_B, 3 * WIDTH_A + WIDTH_B + KV_WIDTH_B]
    for l in range(DEPTH):
        xn = rmsnorm(x, norm1_g[l])
        proj = xn @ w_in[l]
        qa, ka, va, qb, kb, vb = jnp.split(proj, cuts, axis=-1)
        qa = qa.reshape(B, S, N_HEADS_A, HEAD_DIM)
        ka = ka.reshape(B, S, N_HEADS_A, HEAD_DIM)
        va = va.reshape(B, S, N_HEADS_A, HEAD_DIM)
        qb = qb.reshape(B, S, N_HEADS_B, HEAD_DIM)
        kb = kb.reshape(B, S, N_KV_B, HEAD_DIM)
        vb = vb.reshape(B, S, N_KV_B, HEAD_DIM)
        oa = dilated_attention(qa, ka, va, slopes_a).reshape(B, S, WIDTH_A)
        ob, _ = banded_attn(qb, kb, vb, slopes_b, LOCAL_WINDOW, 1, sink[l])
        ob = ob.reshape(B, S, WIDTH_B)
        mixed = jnp.concatenate([rmsnorm(oa, a_out_g[l]), rmsnorm(ob, b_out_g[l])], axis=-1)
        x = x + mixed @ w_out[l]
        x = x + peer(rmsnorm(x, norm2_g[l]), w_pq[l], sub_keys[l], expert_u[l], expert_v[l])
    return rmsnorm(x, normf_g)


def setup_inputs(seed: int = 0) -> dict:
    key = jax.random.key(seed)
    ks = jax.random.split(key, 16)
    f32 = jnp.float32
    D = D_MODEL
    nrm = lambda k, shape, sc: jax.random.normal(k, shape, f32) * sc
    return {
        "x_prompt": nrm(ks[0], (BATCH, SEQ, D), 1.0),
        "x_sample": nrm(ks[1], (DEC_BATCH, DEC_SEQ, D), 1.0),
        "norm1_g": 1.0 + nrm(ks[2], (DEPTH, D), 0.02),
        "w_in": nrm(ks[3], (DEPTH, D, IN_WIDTH), D ** -0.5),
        "a_out_g": 1.0 + nrm(ks[4], (DEPTH, WIDTH_A), 0.02),
        "b_out_g": 1.0 + nrm(ks[5], (DEPTH, WIDTH_B), 0.02),
        "sink": nrm(ks[6], (DEPTH, N_HEADS_B), 0.1),
        "w_out": nrm(ks[7], (DEPTH, MIX_WIDTH, D), MIX_WIDTH ** -0.5),
        "norm2_g": 1.0 + nrm(ks[8], (DEPTH, D), 0.02),
        "w_pq": nrm(ks[9], (DEPTH, D, PEER_HEADS * PEER_DKEY), D ** -0.5),
        "sub_keys": nrm(ks[10], (DEPTH, PEER_HEADS, 2, PEER_NKEYS, PEER_DKEY // 2), (PEER_DKEY // 2) ** -0.5),
        "expert_u": nrm(ks[11], (DEPTH, N_EXPERTS, D), D ** -0.5),
        "expert_v": nrm(ks[12], (DEPTH, N_EXPERTS, D), 0.1),
        "normf_g": 1.0 + nrm(ks[13], (D,), 0.02),
    }


def reference(x_prompt, x_sample, norm1_g, w_in, a_out_g, b_out_g, sink, w_out, norm2_g, w_pq, sub_keys, expert_u, expert_v, normf_g):
    y_prompt = trunk(x_prompt, norm1_g, w_in, a_out_g, b_out_g, sink, w_out, norm2_g, w_pq, sub_keys, expert_u, expert_v, normf_g)
    y_sample = trunk(x_sample, norm1_g, w_in, a_out_g, b_out_g, sink, w_out, norm2_g, w_pq, sub_keys, expert_u, expert_v, normf_g)
    return (y_prompt, y_sample)
```

```python
import math
from contextlib import ExitStack

import numpy as np
import concourse.bass as bass
import concourse.mybir as mybir
from concourse.bass_utils import run_bass_kernel_spmd

F32 = mybir.dt.float32
BF16 = mybir.dt.bfloat16
I32 = mybir.dt.int32
U32 = mybir.dt.uint32
AF = mybir.ActivationFunctionType
ALU = mybir.AluOpType
AX = mybir.AxisListType

NCORES = 8
D = 2048
KC = 16
T_OWN = 3072
HALO = 1024
T_EXT = T_OWN + 2 * HALO
NT_OWN = T_OWN // 128
NT_EXT = T_EXT // 128
SBT = T_EXT // 2
IN_W = 4608
NOC = IN_W // 128
EPS = 1e-6
SEG_B = float(2 ** 17)
NEGM = -30000.0


class _Op:
    __slots__ = ("eng", "fn", "is_dma", "semkey", "deps", "needed", "token", "emitted", "idx")

    def __init__(self, eng, fn, is_dma, semkey):
        self.eng = eng
        self.fn = fn
        self.is_dma = is_dma
        self.semkey = semkey
        self.deps = []
        self.needed = False
        self.token = None
        self.emitted = False
        self.idx = 0


class Tracker:
    ENGS = ("pe", "act", "dve", "pool", "sp")

    def __init__(self, nc):
        self.nc = nc
        self.eng = {"pe": nc.tensor, "act": nc.scalar, "dve": nc.vector,
                    "pool": nc.gpsimd, "sp": nc.sync}
        self.esem = {e: nc.semaphore("s_" + e).__enter__() for e in self.ENGS}
        self.ecount = {e: 0 for e in self.ENGS}
        self.dsem = {}
        self.dlast = {}
        self.pending = []
        self.last_w = {}
        self.readers = {}
        self.known = {e: {} for e in self.ENGS}
        self.last_op = {e: None for e in self.ENGS}
        self.n_ops = 0
        self.n_waits = 0
        self.eidx = {e: 0 for e in self.ENGS}
        self.embed = True

    def _record(self, op, reads, writes):
        if not op.is_dma:
            self.eidx[op.eng] += 1
            op.idx = self.eidx[op.eng]
        deps = []
        for k in reads:
            w = self.last_w.get(k)
            if w is not None:
                deps.append(w)
        for k in writes:
            w = self.last_w.get(k)
            if w is not None:
                deps.append(w)
            deps.extend(self.readers.get(k, ()))
        if op.is_dma:
            prev = self.dlast.get(op.semkey)
            if prev is not None:
                deps.append(prev)
            self.dlast[op.semkey] = op
        seen = set()
        for d in deps:
            if d is op or id(d) in seen:
                continue
            seen.add(id(d))
            if (not d.is_dma) and (not op.is_dma) and d.eng == "pe" and op.eng == "pe":
                continue
            if (not d.is_dma) and (not op.is_dma) and d.eng == op.eng and op.eng in ("dve", "act") \
                    and op.idx - d.idx >= 3:
                continue
            d.needed = True
            op.deps.append(d)
        for k in writes:
            self.last_w[k] = op
            self.readers[k] = []
        for k in reads:
            if k in writes:
                continue
            lst = self.readers.setdefault(k, [])
            if not op.is_dma:
                lst[:] = [r for r in lst if r.is_dma or r.eng != op.eng]
            lst.append(op)
        self.pending.append(op)
        if not op.is_dma:
            self.last_op[op.eng] = op
        self.n_ops += 1
        return op

    def op(self, eng, fn, reads=(), writes=()):
        return self._record(_Op(eng, fn, False, None), tuple(reads), tuple(writes))

    def dma(self, eng, fn, semkey, reads=(), writes=()):
        return self._record(_Op(eng, fn, True, semkey), tuple(reads), tuple(writes))

    def _wait(self, engname, sem, cnt):
        kn = self.known[engname]
        key = id(sem)
        if kn.get(key, 0) >= cnt:
            return
        self.eng[engname].wait_ge(sem, cnt)
        kn[key] = cnt
        self.n_waits += 1

    def flush(self):
        for e in self.ENGS:
            if self.last_op[e] is not None and not self.last_op[e].emitted:
                self.last_op[e].needed = True
        for op in self.pending:
            need = {}
            for d in op.deps:
                assert d.emitted and d.token is not None, "dep without token"
                sem, cnt = d.token
                if need.get(id(sem), (None, 0))[1] < cnt:
                    need[id(sem)] = (sem, cnt)
            kn = self.known[op.eng]
            todo = [(sem, cnt) for sem, cnt in need.values() if kn.get(id(sem), 0) < cnt]
            emb = None
            if todo and self.embed and (not op.is_dma) and op.eng in ("pe", "act", "dve"):
                emb = todo.pop()
            for sem, cnt in todo:
                self._wait(op.eng, sem, cnt)
            ins = op.fn(self.eng[op.eng])
            if emb is not None:
                ins._wait_ge(emb[0], emb[1])
                kn[id(emb[0])] = emb[1]
                self.n_waits += 1
            if op.is_dma:
                ent = self.dsem.get(op.semkey)
                if ent is None:
                    ent = [self.nc.semaphore("d_%d" % len(self.dsem)).__enter__(), 0]
                    self.dsem[op.semkey] = ent
                ent[1] += 16
                ins.then_inc(ent[0], 16)
                op.token = (ent[0], ent[1])
            elif op.needed:
                self.ecount[op.eng] += 1
                ins.then_inc(self.esem[op.eng], 1)
                op.token = (self.esem[op.eng], self.ecount[op.eng])
            op.emitted = True
            op.fn = None
        self.pending = []

    def barrier(self):
        self.flush()
        for e in self.ENGS:
            for e2 in self.ENGS:
                if self.ecount[e2] > 0:
                    self._wait(e, self.esem[e2], self.ecount[e2])
            for sem, cnt in self.dsem.values():
                if cnt > 0:
                    self._wait(e, sem, cnt)
        self.last_w = {}
        self.readers = {}
        self.dlast = {}
        self.last_op = {e: None for e in self.ENGS}


def cs(start, n, step):
    return slice(start, start + (n - 1) * step + 1, step)


def alibi_slopes():
    s = [2.0 ** (-8.0 * (i + 1) / 32.0) for i in range(32)]
    return s[0::2], s[1::2]


def build_program(debug=False, phases="ABCDE"):
    nc = bass.Bass("TRN2", target_bir_lowering=False)
    T = Tracker(nc)
    skind = "ExternalOutput" if debug else "Internal"

    def din(name, shape, dt=F32):
        return nc.dram_tensor(name, shape, dt, kind="ExternalInput").ap()

    x_ext = din("x_ext", [T_EXT, D])
    segq_d = din("segq", [3, T_EXT])
    segk_d = din("segk", [3, T_EXT])
    g1_d = din("g1b", [128, D])
    g2_d = din("g2b", [128, D])
    gf_d = din("gfb", [128, D])
    gab_d = din("gab", [128, 16])
    sink_d = din("sinkc", [128, 16])
    w_in_d = din("w_in", [D, IN_W])
    w_out_d = din("w_out", [D, D])
    w_pq_d = din("w_pq", [D, D])
    sk_d = din("sub_keys", [16 * 128, 128])
    eu_d = din("expert_u", [16384, D])
    ev_d = din("expert_v", [16384, D])
    y_d = nc.dram_tensor("y", [T_OWN, D], F32, kind="ExternalOutput").ap()
    qkvT_d = nc.dram_tensor("qkvT", [IN_W, T_EXT], BF16, kind=skind).ap()
    oaT_d = nc.dram_tensor("oaT", [D, T_OWN], F32, kind=skind).ap()
    rst_d = nc.dram_tensor("rst", [2, 128, T_OWN], F32, kind=skind).ap()
    x1_d = nc.dram_tensor("x1", [T_OWN, D], F32, kind=skind).ap()
    scr_d = nc.dram_tensor("scr", [T_OWN, D], F32, kind="Internal").ap()
    eub_d = nc.dram_tensor("eub", [16384, D], BF16, kind="Internal").ap()
    evb_d = nc.dram_tensor("evb", [16384, D], BF16, kind="Internal").ap()
    if debug:
        dbg_d = nc.dram_tensor("dbg", [NT_OWN, 128, 512], F32, kind="ExternalOutput").ap()

    gs = ExitStack()

    def gsb(name, shape, dt=F32):
        return gs.enter_context(nc.sbuf_tensor(name, shape, dt))

    PSB = [gs.enter_context(nc.psum_tensor("psb%d" % i, [128, 1024], BF16)) for i in range(2)]
    PSF = [gs.enter_context(nc.psum_tensor("psf%d" % i, [128, 512], F32)) for i in range(6)]

    ident = gsb("ident", [128, 128], BF16)
    ones128 = gsb("ones128", [128, 128], BF16)
    onesA = gsb("onesA", [128, 128], BF16)
    onesB = gsb("onesB", [128, 128], BF16)
    epsc = gsb("epsc", [128, 1], F32)
    with ExitStack() as es:
        idi = es.enter_context(nc.sbuf_tensor("idi", [128, 128], I32))
        idf = es.enter_context(nc.sbuf_tensor("idf", [128, 128], F32))
        T.op("pool", lambda e: e.iota(idi[:], [[1, 128]], base=0, channel_multiplier=-1), writes=["idi"])
        T.op("dve", lambda e: e.tensor_copy(out=idf[:], in_=idi[:]), reads=["idi"], writes=["idf"])
        T.op("dve", lambda e: e.tensor_single_scalar(out=ident[:], in_=idf[:], scalar=0.0, op=ALU.is_equal),
             reads=["idf"], writes=["ident"])
        T.op("pool", lambda e: e.memset(ones128[:], 1.0), writes=["ones128"])
        T.op("pool", lambda e: e.memset(onesA[:], 0.0), writes=["onesA"])
        T.op("pool", lambda e: e.memset(onesA[:, 0:64], 1.0), reads=["onesA"], writes=["onesA"])
        T.op("pool", lambda e: e.memset(onesB[:], 0.0), writes=["onesB"])
        T.op("pool", lambda e: e.memset(onesB[:, 64:128], 1.0), reads=["onesB"], writes=["onesB"])
        T.op("pool", lambda e: e.memset(epsc[:], EPS), writes=["epsc"])
        T.barrier()

    def rms_scale(xi, xkey, ssq_col, rs_col, key, nfeat, junk):
        T.op("act", lambda e: e.activation(out=junk[:], in_=xi[:], func=AF.Square, accum_out=ssq_col),
             reads=[xkey, key + "z"], writes=[key + "s"])
        T.op("act", lambda e: e.activation(out=rs_col, in_=ssq_col, func=AF.Sqrt, scale=1.0 / nfeat, bias=epsc[:, 0:1]),
             reads=[key + "s"], writes=[key + "r"])
        T.op("dve", lambda e: e.reciprocal(out=rs_col, in_=rs_col), reads=[key + "r"], writes=[key + "r"])

    def transpose16(src, srckey, dst_fn, dstkey):
        for half in range(2):
            pb = PSB[half]
            for k8 in range(8):
                k = half * 8 + k8
                T.op("pe", lambda e, pb=pb, k8=k8, k=k: e.transpose(pb[:, k8 * 128:(k8 + 1) * 128],
                                                                   in_=src[:, k * 128:(k + 1) * 128], identity=ident[:]),
                     reads=[srckey], writes=["psb%d" % half])
            dst = dst_fn(half)
            if half == 0:
                T.op("act", lambda e, pb=pb, dst=dst: e.activation(out=dst, in_=pb[:].rearrange("p (k t) -> p k t", k=8), func=AF.Copy),
                     reads=["psb%d" % half], writes=[dstkey])
            else:
                T.op("dve", lambda e, pb=pb, dst=dst: e.tensor_copy(out=dst, in_=pb[:].rearrange("p (k t) -> p k t", k=8)),
                     reads=["psb%d" % half], writes=[dstkey])

    def phase_A():
        with ExitStack() as es:
            def sb(name, shape, dt=F32):
                return es.enter_context(nc.sbuf_tensor(name, shape, dt))
            xnT = sb("xnT", [128, KC, SBT], BF16)
            xin = [sb("xinA%d" % i, [128, D], F32) for i in range(2)]
            xnb = [sb("xnbA%d" % i, [128, D], BF16) for i in range(2)]
            junk = sb("junkA", [128, D], BF16)
            ssq = sb("ssqA1", [128, NT_EXT])
            rstd = sb("rstdA1", [128, NT_EXT])
            wb = [sb("wbA%d" % i, [128, KC, 128], BF16) for i in range(3)]
            stg = [sb("stgA%d" % i, [128, SBT], BF16) for i in range(2)]
            g1 = sb("g1A", [128, D])
            T.dma("sp", lambda e: e.dma_start(out=g1[:], in_=g1_d[:, :]), "ldg", writes=["g1"])
            T.op("pool", lambda e: e.memset(ssq[:], 0.0), writes=["ssqz"])
            win = w_in_d.rearrange("(kc p) c -> p kc c", p=128)
            for sbi in range(2):
                for tl in range(SBT // 128):
                    t = sbi * (SBT // 128) + tl
                    xi = xin[t % 2]
                    xb = xnb[t % 2]
                    xk = "xin%d" % (t % 2)
                    bk = "xnb%d" % (t % 2)
                    T.dma("sp", lambda e, xi=xi, t=t: e.dma_start(out=xi[:], in_=x_ext[t * 128:(t + 1) * 128, :]),
                          xk, writes=[xk])
                    T.op("act", lambda e, xi=xi, t=t: e.activation(out=junk[:], in_=xi[:], func=AF.Square, accum_out=ssq[:, t:t + 1]),
                         reads=[xk, "ssqz"], writes=["ssq%d" % t])
                    T.op("act", lambda e, t=t: e.activation(out=rstd[:, t:t + 1], in_=ssq[:, t:t + 1], func=AF.Sqrt,
                                                            scale=1.0 / D, bias=epsc[:, 0:1]),
                         reads=["ssq%d" % t], writes=["rstd%d" % t])
                    T.op("dve", lambda e, t=t: e.reciprocal(out=rstd[:, t:t + 1], in_=rstd[:, t:t + 1]),
                         reads=["rstd%d" % t], writes=["rstd%d" % t])
                    T.op("dve", lambda e, xi=xi, xb=xb, t=t: e.scalar_tensor_tensor(out=xb[:], in0=xi[:], scalar=rstd[:, t:t + 1], in1=g1[:],
                                                                                    op0=ALU.mult, op1=ALU.mult),
                         reads=[xk, "rstd%d" % t, "g1"], writes=[bk])
                    transpose16(xb, bk, lambda half, tl=tl: xnT[:, half * 8:half * 8 + 8, tl * 128:(tl + 1) * 128], "xnT%d" % tl)
                for oc in range(NOC):
                    w = wb[oc % 3]
                    wk = "wb%d" % (oc % 3)
                    T.dma("pool", lambda e, w=w, oc=oc: e.dma_start(out=w[:], in_=win[:, :, oc * 128:(oc + 1) * 128]), wk, writes=[wk])
                    st = stg[oc % 2]
                    skeys = []
                    is_q = (oc < 8) or (24 <= oc < 32)
                    tgs = list(range(SBT // 512))
                    if is_q:
                        tgs = [2, 3, 4] if sbi == 0 else [0, 1, 2]
                    for tg in tgs:
                        bi = (oc * (SBT // 512) + tg) % 4
                        acc = PSF[bi]
                        xkeys = ["xnT%d" % (tg * 4 + i) for i in range(4)]
                        for kc in range(KC):
                            T.op("pe", lambda e, acc=acc, w=w, kc=kc, tg=tg: e.matmul(acc[:, 0:512], lhsT=w[:, kc, :],
                                                                                   rhs=xnT[:, kc, tg * 512:(tg + 1) * 512],
                                                                                   start=(kc == 0), stop=(kc == KC - 1)),
                                 reads=[wk] + xkeys, writes=["psf%d" % bi])
                        sk_ = "stg%d_%d" % (oc % 2, tg)
                        skeys.append(sk_)
                        if tg % 2 == 0:
                            T.op("act", lambda e, acc=acc, st=st, tg=tg: e.activation(out=st[:, tg * 512:(tg + 1) * 512], in_=acc[:, 0:512], func=AF.Copy),
                                 reads=["psf%d" % bi], writes=[sk_])
                        else:
                            T.op("dve", lambda e, acc=acc, st=st, tg=tg: e.tensor_copy(out=st[:, tg * 512:(tg + 1) * 512], in_=acc[:, 0:512]),
                                 reads=["psf%d" % bi], writes=[sk_])
                    c_lo, c_hi = tgs[0] * 512, (tgs[-1] + 1) * 512
                    T.dma("sp", lambda e, st=st, oc=oc, sbi=sbi, c_lo=c_lo, c_hi=c_hi: e.dma_start(
                        out=qkvT_d[oc * 128:(oc + 1) * 128, sbi * SBT + c_lo:sbi * SBT + c_hi], in_=st[:, c_lo:c_hi]),
                          "stgo%d" % (oc % 2), reads=skeys)
            T.barrier()

    def phase_B():
        slopes_a, slopes_b = alibi_slopes()
        with ExitStack() as es:
            def sb(name, shape, dt=F32):
                return es.enter_context(nc.sbuf_tensor(name, shape, dt))
            QT = [[sb("QT%d_%d" % (s_, j), [128, T_OWN], BF16) for j in range(2)] for s_ in range(2)]
            KT = [[sb("KT%d_%d" % (s_, j), [128, T_EXT], BF16) for j in range(2)] for s_ in range(2)]
            VT = [sb("VT%d" % i, [128, T_EXT], BF16) for i in range(2)]
            OD = [sb("OD%d" % j, [128, T_OWN]) for j in range(2)]
            rbc = sb("rbcB", [64, T_OWN])
            sqh = [sb("sqB%d" % j, [64, T_OWN], BF16) for j in range(2)]
            ssq = [sb("ssqB%d" % i, [128, T_OWN]) for i in range(2)]
            onesf = sb("onesfB", [128, 64])
            cvals = (-64, 64, -128, 0, 128)
            AR = sb("ARc", [128, 5, 384])
            MK = sb("MKc", [128, 5, 384])
            ari = sb("ari", [128, 384], I32)
            biasT = sb("biasT", [128, 12, 384])
            tmp = [sb("tmpB%d" % i, [128, 384]) for i in range(4)]
            PT = [sb("PTB%d" % i, [128, 384], BF16) for i in range(4)]
            Vaug = [sb("Vaug%d" % i, [128, 2, 65], BF16) for i in range(2)]
            esink = sb("esink", [128, 16])

            for s_ in range(2):
                for j in range(2):
                    T.dma("pool", lambda e, s_=s_, j=j: e.dma_start(out=QT[s_][j][64:67, :], in_=segq_d[:, HALO:HALO + T_OWN]),
                          "seg%d" % (s_ * 2 + j), writes=["QTseg"])
                    T.dma("pool", lambda e, s_=s_, j=j: e.dma_start(out=KT[s_][j][64:67, :], in_=segk_d[:, :]),
                          "segk%d" % (s_ * 2 + j), writes=["KTseg"])
            conv_jobs = []
            for ti, (src, dst) in enumerate(((eu_d, eub_d), (ev_d, evb_d))):
                sv = src.rearrange("(p r) d -> p r d", p=128)
                dv = dst.rearrange("(p r) d -> p r d", p=128)
                for c in range(32):
                    conv_jobs.append((sv, dv, c))
            conv_state = [0, 0]

            def conv_tick():
                conv_state[1] += 1
                if conv_state[1] % 14 == 0 and conv_state[0] < len(conv_jobs):
                    sv, dv, c = conv_jobs[conv_state[0]]
                    T.dma("pool", lambda e: e.dma_start(out=dv[:, c * 4:(c + 1) * 4, :], in_=sv[:, c * 4:(c + 1) * 4, :]),
                          "cv%d" % (conv_state[0] % 4))
                    conv_state[0] += 1

            T.dma("sp", lambda e: e.dma_start(out=esink[:], in_=sink_d[:, :]), "sink", writes=["esink"])
            T.op("act", lambda e: e.activation(out=esink[:], in_=esink[:], func=AF.Exp), reads=["esink"], writes=["esink"])
            T.op("pool", lambda e: e.memset(onesf[:], 1.0), writes=["onesf"])
            for i in range(2):
                T.op("pool", lambda e, i=i: e.memset(Vaug[i][:], 1.0), writes=["Vaug%d" % i])
                T.op("pool", lambda e, i=i: e.memset(ssq[i][:], 0.0), writes=["ssqB%d" % i])
            for ci, c in enumerate(cvals):
                W = 64.0 if ci < 2 else 128.0
                T.op("pool", lambda e, c=c: e.iota(ari[:], [[-1, 384]], base=c, channel_multiplier=1), reads=["ari"], writes=["ari"])
                T.op("dve", lambda e, ci=ci: e.tensor_copy(out=AR[:, ci, :], in_=ari[:]), reads=["ari"], writes=["AR%d" % ci])
                T.op("act", lambda e, ci=ci: e.activation(out=AR[:, ci, :], in_=AR[:, ci, :], func=AF.Abs), reads=["AR%d" % ci], writes=["AR%d" % ci])
                T.op("dve", lambda e, ci=ci, W=W: e.tensor_scalar(out=MK[:, ci, :], in0=AR[:, ci, :], scalar1=W, scalar2=NEGM,
                                                                 op0=ALU.is_gt, op1=ALU.mult),
                     reads=["AR%d" % ci], writes=["MK%d" % ci])

            def load_hp(hp):
                slot = hp % 2
                vk = "VT%d" % slot
                if hp < 8:
                    for j in range(2):
                        T.dma("sp", lambda e, j=j: e.dma_start(out=QT[slot][j][0:64, :], in_=qkvT_d[hp * 128 + 64 * j:hp * 128 + 64 * j + 64, HALO:HALO + T_OWN]),
                              "QT%d_%d" % (slot, j), writes=["QT%d_%d" % (slot, j)])
                        T.dma("sp", lambda e, j=j: e.dma_start(out=KT[slot][j][0:64, :], in_=qkvT_d[(8 + hp) * 128 + 64 * j:(8 + hp) * 128 + 64 * j + 64, :]),
                              "KT%d_%d" % (slot, j), writes=["KT%d_%d" % (slot, j)])
                    T.dma("sp", lambda e: e.dma_start(out=VT[slot][:], in_=qkvT_d[(16 + hp) * 128:(17 + hp) * 128, :]), vk, writes=[vk])
                else:
                    hb = hp - 8
                    kv = hb // 2
                    kr = 4096 + 64 * kv
                    vr = 4352 + 64 * kv
                    for j in range(2):
                        T.dma("sp", lambda e, j=j: e.dma_start(out=QT[slot][j][0:64, :], in_=qkvT_d[(24 + hb) * 128 + 64 * j:(24 + hb) * 128 + 64 * j + 64, HALO:HALO + T_OWN]),
                              "QT%d_%d" % (slot, j), writes=["QT%d_%d" % (slot, j)])
                        T.dma("sp", lambda e, j=j: e.dma_start(out=KT[slot][j][0:64, :], in_=qkvT_d[kr:kr + 64, :]),
                              "KT%d_%d" % (slot, j), writes=["KT%d_%d" % (slot, j)])
                        T.dma("sp", lambda e, j=j: e.dma_start(out=VT[slot][64 * j:64 * j + 64, :], in_=qkvT_d[vr:vr + 64, :]), vk + "_%d" % j, writes=[vk])

            it = 0
            load_hp(0)
            for hp in range(16):
                isB = hp >= 8
                slot = hp % 2
                vk = "VT%d" % slot
                if hp + 1 < 16:
                    load_hp(hp + 1)
                if not isB:
                    pats = [(1, 64, 0, 1), (4, 64, 0, 1), (16, 64, 0, 1)]
                    slopes = [slopes_a[2 * hp], slopes_a[2 * hp + 1]]
                    heads = [2 * hp, 2 * hp + 1]
                else:
                    hb = hp - 8
                    pats = [(1, 128, 2, 3)]
                    slopes = [slopes_b[2 * hb], slopes_b[2 * hb + 1]]
                    heads = [2 * hb, 2 * hb + 1]
                bidx = {}
                nb = 0
                for j in range(2):
                    for pi, (d, W, c0, c1) in enumerate(pats):
                        cis = (c0, c1) if not isB else (2, 3, 4)
                        for ci in cis:
                            bidx[(j, pi, ci)] = nb
                            T.op("dve", lambda e, nb=nb, ci=ci, sl=-slopes[j] * d: e.scalar_tensor_tensor(
                                out=biasT[:, nb, :], in0=AR[:, ci, :], scalar=sl, in1=MK[:, ci, :], op0=ALU.mult, op1=ALU.add),
                                 reads=["AR%d" % ci, "MK%d" % ci], writes=["bias%d" % nb])
                            nb += 1
                for j in range(2):
                    T.op("pool", lambda e, j=j: e.memset(OD[j][:], 0.0), writes=["OD%d" % j])
                iters = []
                for pi, (d, W, c0, c1) in enumerate(pats):
                    ua = HALO // d
                    L = T_OWN // d
                    ub = ua + L
                    nkt = -(-(L + 2 * W) // 128)
                    for r in range(d):
                        for m in range(nkt):
                            k0 = ua - W + 128 * m
                            nk = min(128, ub + W - k0)
                            qlo = max(ua, k0 - W)
                            qhi = min(ub, k0 + nk + W)
                            nq = qhi - qlo
                            ci = cvals.index(k0 - qlo)
                            iters.append((pi, nk, nq, ci, cs(r + d * k0, nk, d), cs(r + d * qlo - HALO, nq, d)))

                def stage1(itn, desc, slot=slot, vk=vk, bidx=bidx):
                    pi, nk, nq, ci, kcols, qcols = desc
                    pb = PSB[itn % 2]
                    pbk = "psb%d" % (itn % 2)
                    va = Vaug[itn % 2]
                    vak = "Vaug%d" % (itn % 2)
                    T.op("pe", lambda e: e.transpose(pb[0:nk, 0:128], in_=VT[slot][:, kcols], identity=ident[:]),
                         reads=[vk], writes=[pbk])
                    T.op("act", lambda e: e.activation(out=va[0:nk, :, 0:64], in_=pb[0:nk, 0:128].rearrange("p (a b) -> p a b", b=64), func=AF.Copy),
                         reads=[pbk], writes=[vak])
                    for j in range(2):
                        si = (2 * itn + j) % 4
                        S = PSF[si]
                        sk_ = "psf%d" % si
                        tm = tmp[si]
                        pt = PT[si]
                        T.op("pe", lambda e, S=S, j=j: e.matmul(
                            S[0:nk, 0:nq], lhsT=KT[slot][j][0:67, kcols], rhs=QT[slot][j][0:67, qcols], start=True, stop=True),
                             reads=["KT%d_%d" % (slot, j), "QT%d_%d" % (slot, j), "KTseg", "QTseg"], writes=[sk_])
                        bi = bidx[(j, pi, ci)]
                        T.op("dve", lambda e, S=S, tm=tm, bi=bi: e.scalar_tensor_tensor(
                            out=tm[0:nk, 0:nq], in0=S[0:nk, 0:nq], scalar=0.125, in1=biasT[0:nk, bi, 0:nq], op0=ALU.mult, op1=ALU.add),
                             reads=[sk_, "bias%d" % bi], writes=["tmp%d" % si])
                        T.op("act", lambda e, tm=tm, pt=pt: e.activation(out=pt[0:nk, 0:nq], in_=tm[0:nk, 0:nq], func=AF.Exp),
                             reads=["tmp%d" % si], writes=["PT%d" % si])

                def stage2(itn, desc):
                    pi, nk, nq, ci, kcols, qcols = desc
                    va = Vaug[itn % 2]
                    vak = "Vaug%d" % (itn % 2)
                    for j in range(2):
                        si = (2 * itn + j) % 4
                        pt = PT[si]
                        Ops = PSF[4 + j]
                        opk = "psf%d" % (4 + j)
                        T.op("pe", lambda e, j=j, pt=pt, Ops=Ops: e.matmul(Ops[0:65, 0:nq], lhsT=va[0:nk, j, :], rhs=pt[0:nk, 0:nq], start=True, stop=True),
                             reads=[vak, "PT%d" % si], writes=[opk])
                        T.op("dve", lambda e, j=j, Ops=Ops: e.tensor_tensor(out=OD[j][0:65, qcols], in0=Ops[0:65, 0:nq], in1=OD[j][0:65, qcols], op=ALU.add),
                             reads=[opk, "OD%d" % j], writes=["OD%d" % j])

                stage1(it, iters[0])
                for i_ in range(len(iters)):
                    if i_ + 1 < len(iters):
                        stage1(it + i_ + 1, iters[i_ + 1])
                    stage2(it + i_, iters[i_])
                    conv_tick()
                it += len(iters)
                sidx = 1 if isB else 0
                for j in range(2):
                    odk = "OD%d" % j
                    if isB:
                        hh = heads[j]
                        T.op("dve", lambda e, j=j, hh=hh: e.tensor_scalar(out=OD[j][64:65, :], in0=OD[j][64:65, :], scalar1=esink[64:65, hh:hh + 1],
                                                                         scalar2=None, op0=ALU.add),
                             reads=[odk, "esink"], writes=[odk])
                    for g in range(T_OWN // 512):
                        pp = PSF[g % 4]
                        ppk = "psf%d" % (g % 4)
                        T.op("pe", lambda e, j=j, pp=pp, g=g: e.matmul(pp[0:64, 0:512], lhsT=onesf[64:65, 0:64], rhs=OD[j][64:65, g * 512:(g + 1) * 512],
                                                                      start=True, stop=True),
                             reads=[odk, "onesf"], writes=[ppk])
                        T.op("dve", lambda e, pp=pp, g=g: e.reciprocal(out=rbc[:, g * 512:(g + 1) * 512], in_=pp[0:64, 0:512]),
                             reads=[ppk], writes=["rbc%d" % g])
                    rkeys = ["rbc%d" % g for g in range(T_OWN // 512)]
                    T.op("dve", lambda e, j=j: e.tensor_tensor(out=OD[j][0:64, :], in0=OD[j][0:64, :], in1=rbc[:, :], op=ALU.mult),
                         reads=[odk] + rkeys, writes=[odk])
                    T.op("act", lambda e, j=j: e.activation(out=sqh[j][:], in_=OD[j][0:64, :], func=AF.Square), reads=[odk], writes=["sq%d" % j])
                    T.dma("sp", lambda e, j=j, hp=hp: e.dma_start(out=oaT_d[hp * 128 + 64 * j:hp * 128 + 64 * j + 64, :], in_=OD[j][0:64, :]), "oaT%d" % j, reads=[odk])
                for g in range(T_OWN // 512):
                    pp = PSF[4 + g % 2]
                    ppk = "psf%d" % (4 + g % 2)
                    for j in range(2):
                        T.op("pe", lambda e, pp=pp, g=g, j=j: e.matmul(pp[:, 0:512], lhsT=ones128[0:64, :], rhs=sqh[j][:, g * 512:(g + 1) * 512],
                                                                      start=(j == 0), stop=(j == 1)),
                             reads=["sq%d" % j], writes=[ppk])
                    T.op("dve", lambda e, pp=pp, g=g, sidx=sidx: e.tensor_tensor(out=ssq[sidx][:, g * 512:(g + 1) * 512], in0=pp[:, 0:512],
                                                                                 in1=ssq[sidx][:, g * 512:(g + 1) * 512], op=ALU.add),
                         reads=[ppk, "ssqB%d" % sidx], writes=["ssqB%d" % sidx])
            while conv_state[0] < len(conv_jobs):
                conv_state[1] = 13
                conv_tick()
            for i in range(2):
                T.op("act", lambda e, i=i: e.activation(out=ssq[i][:], in_=ssq[i][:], func=AF.Sqrt, scale=1.0 / 1024.0, bias=epsc[:, 0:1]),
                     reads=["ssqB%d" % i], writes=["ssqB%d" % i])
                T.op("dve", lambda e, i=i: e.reciprocal(out=ssq[i][:], in_=ssq[i][:]), reads=["ssqB%d" % i], writes=["ssqB%d" % i])
                T.dma("sp", lambda e, i=i: e.dma_start(out=rst_d[i], in_=ssq[i][:]), "rst%d" % i, reads=["ssqB%d" % i])
            T.barrier()

    def phase_C():
        with ExitStack() as es:
            def sb(name, shape, dt=F32):
                return es.enter_context(nc.sbuf_tensor(name, shape, dt))
            wout = sb("wout", [128, KC, D], BF16)
            rst = sb("rstC", [128, 2, T_OWN])
            gab = sb("gabC", [128, 16])
            ld = sb("ldC", [128, KC, 512])
            mixedT = [sb("mixT%d" % i, [128, KC, 512], BF16) for i in range(2)]
            xin = [sb("xinC%d" % i, [128, D]) for i in range(2)]
            x1t = [sb("x1tC%d" % i, [128, D]) for i in range(2)]
            for kc in range(KC):
                for h in range(2):
                    T.dma("pool", lambda e, kc=kc, h=h: e.dma_start(out=wout[:, kc, h * 1024:(h + 1) * 1024],
                                                                      in_=w_out_d[kc * 128:(kc + 1) * 128, h * 1024:(h + 1) * 1024]),
                          "wo%d" % ((2 * kc + h) % 4), writes=["wout%d_%d" % (kc, h)])
            woutkeys = ["wout%d_%d" % (kc, h) for kc in range(KC) for h in range(2)]
            for i in range(2):
                T.dma("sp", lambda e, i=i: e.dma_start(out=rst[:, i, :], in_=rst_d[i]), "rstl%d" % i, writes=["rst"])
            T.dma("sp", lambda e: e.dma_start(out=gab[:], in_=gab_d[:, :]), "gab", writes=["gab"])
            oaT_v = oaT_d.rearrange("(c p) t -> p c t", p=128)
            for tg in range(T_OWN // 512):
                T.dma("sp", lambda e, tg=tg: e.dma_start(out=ld[:], in_=oaT_v[:, :, tg * 512:(tg + 1) * 512]), "ldC", writes=["ld"])
                mx = mixedT[tg % 2]
                mk_ = "mix%d" % (tg % 2)
                for c in range(16):
                    T.op("dve", lambda e, mx=mx, c=c, tg=tg: e.scalar_tensor_tensor(out=mx[:, c, :], in0=ld[:, c, :], scalar=gab[:, c:c + 1],
                                                                                  in1=rst[:, c // 8, tg * 512:(tg + 1) * 512], op0=ALU.mult, op1=ALU.mult),
                         reads=["ld", "gab", "rst"], writes=[mk_ + "_%d" % c])
                mkeys = [mk_ + "_%d" % c for c in range(16)]
                for tl in range(4):
                    t = tg * 4 + tl
                    xi = xin[t % 2]
                    xk = "xinC%d" % (t % 2)
                    xo = x1t[t % 2]
                    xok = "x1t%d" % (t % 2)
                    T.dma("sp", lambda e, xi=xi, t=t: e.dma_start(out=xi[:], in_=x_ext[HALO + t * 128:HALO + (t + 1) * 128, :]), xk, writes=[xk])
                    okeys = []
                    for ng in range(4):
                        bi = (t * 4 + ng) % 6
                        acc = PSF[bi]
                        for kc in range(KC):
                            T.op("pe", lambda e, acc=acc, mx=mx, kc=kc, tl=tl, ng=ng: e.matmul(acc[:, 0:512], lhsT=mx[:, kc, tl * 128:(tl + 1) * 128],
                                                                                           rhs=wout[:, kc, ng * 512:(ng + 1) * 512],
                                                                                           start=(kc == 0), stop=(kc == KC - 1)),
                                 reads=mkeys + woutkeys, writes=["psf%d" % bi])
                        T.op("dve", lambda e, acc=acc, xo=xo, xi=xi, ng=ng: e.tensor_tensor(out=xo[:, ng * 512:(ng + 1) * 512], in0=acc[:, 0:512],
                                                                                          in1=xi[:, ng * 512:(ng + 1) * 512], op=ALU.add),
                             reads=["psf%d" % bi, xk], writes=[xok + "_%d" % ng])
                        okeys.append(xok + "_%d" % ng)
                    T.dma("sp", lambda e, xo=xo, t=t: e.dma_start(out=x1_d[t * 128:(t + 1) * 128, :], in_=xo[:]), "x1o%d" % (t % 2), reads=okeys)
            T.barrier()

    def phase_DE(do_E=True):
        with ExitStack() as es_o:
            def osb(name, shape, dt=F32):
                return es_o.enter_context(nc.sbuf_tensor(name, shape, dt))
            eid2 = osb("eid2", [128, 3, 128], I32)
            gat2 = osb("gat2", [128, 3, 128], F32)
            x1j = osb("x1jD", [128, D], F32)
            x1k = osb("x1kD", [128, D], F32)
            g2 = osb("g2D", [128, D], F32)
            x1i = osb("x1iD", [128, D], F32)
            ssq = osb("ssqD", [128, 3 * NT_OWN], F32)
            rs = osb("rsD", [128, 3 * NT_OWN], F32)
            junk = osb("junkD", [128, D], BF16)
            T.dma("sp", lambda e: e.dma_start(out=g2[:], in_=g2_d[:, :]), "g2", writes=["g2"])
            T.op("pool", lambda e: e.memset(ssq[:], 0.0), writes=["ssqDz"])
            with ExitStack() as es:
                def sb(name, shape, dt=F32):
                    return es.enter_context(nc.sbuf_tensor(name, shape, dt))
                wpq = sb("wpq", [128, KC, D], BF16)
                skb = sb("skb", [128, 16, 128], BF16)
                skT = sb("skT", [128, 16, 128], BF16)
                hnb = sb("hnbD", [128, D], BF16)
                hnT = sb("hnTD", [128, KC, 512], BF16)
                qT = sb("qTD", [128, 16, 512], BF16)
                scs = [sb("scsD%d" % i, [128, D]) for i in range(2)]
                for kc in range(KC):
                    for h in range(2):
                        T.dma("pool", lambda e, kc=kc, h=h: e.dma_start(out=wpq[:, kc, h * 1024:(h + 1) * 1024],
                                                                          in_=w_pq_d[kc * 128:(kc + 1) * 128, h * 1024:(h + 1) * 1024]),
                              "wq%d" % ((2 * kc + h) % 4), writes=["wpq%d_%d" % (kc, h)])
                wpqkeys = ["wpq%d_%d" % (kc, h) for kc in range(KC) for h in range(2)]
                T.dma("pool", lambda e: e.dma_start(out=skb[:], in_=sk_d.rearrange("(c n) d -> n c d", n=128)), "skb", writes=["skb"])
                for half in range(2):
                    pb = PSB[half]
                    for k8 in range(8):
                        c = half * 8 + k8
                        T.op("pe", lambda e, pb=pb, k8=k8, c=c: e.transpose(pb[:, k8 * 128:(k8 + 1) * 128], in_=skb[:, c, :], identity=ident[:]),
                             reads=["skb"], writes=["psb%d" % half])
                    T.op("dve", lambda e, pb=pb, half=half: e.tensor_copy(out=skT[:, half * 8:half * 8 + 8, :], in_=pb[:].rearrange("p (k t) -> p k t", k=8)),
                         reads=["psb%d" % half], writes=["skT%d" % half])
                for tgp in range(NT_OWN // 4):
                    for tl in range(4):
                        t = tgp * 4 + tl
                        T.dma("sp", lambda e, t=t: e.dma_start(out=x1i[:], in_=x1_d[t * 128:(t + 1) * 128, :]), "x1i", writes=["x1i"])
                        T.op("act", lambda e, t=t: e.activation(out=junk[:], in_=x1i[:], func=AF.Square, accum_out=ssq[:, t:t + 1]),
                             reads=["x1i", "ssqDz"], writes=["ssqD%d" % t])
                        T.op("act", lambda e, t=t: e.activation(out=rs[:, t:t + 1], in_=ssq[:, t:t + 1], func=AF.Sqrt, scale=1.0 / D, bias=epsc[:, 0:1]),
                             reads=["ssqD%d" % t], writes=["rsD%d" % t])
                        T.op("dve", lambda e, t=t: e.reciprocal(out=rs[:, t:t + 1], in_=rs[:, t:t + 1]), reads=["rsD%d" % t], writes=["rsD%d" % t])
                        T.op("dve", lambda e, t=t: e.scalar_tensor_tensor(out=hnb[:], in0=x1i[:], scalar=rs[:, t:t + 1], in1=g2[:], op0=ALU.mult, op1=ALU.mult),
                             reads=["x1i", "rsD%d" % t, "g2"], writes=["hnb"])
                        transpose16(hnb, "hnb", lambda half, tl=tl: hnT[:, half * 8:half * 8 + 8, tl * 128:(tl + 1) * 128], "hnT%d" % tl)
                    hkeys = ["hnT%d" % tl for tl in range(4)]
                    for c in range(16):
                        bi = c % 4
                        acc = PSF[bi]
                        for kc in range(KC):
                            T.op("pe", lambda e, acc=acc, c=c, kc=kc: e.matmul(acc[:, 0:512], lhsT=wpq[:, kc, c * 128:(c + 1) * 128],
                                                                             rhs=hnT[:, kc, :], start=(kc == 0), stop=(kc == KC - 1)),
                                 reads=wpqkeys + hkeys, writes=["psf%d" % bi])
                        if c % 2 == 0:
                            T.op("act", lambda e, acc=acc, c=c: e.activation(out=qT[:, c, :], in_=acc[:, 0:512], func=AF.Copy),
                                 reads=["psf%d" % bi], writes=["qT%d" % c])
                        else:
                            T.op("dve", lambda e, acc=acc, c=c: e.tensor_copy(out=qT[:, c, :], in_=acc[:, 0:512]),
                                 reads=["psf%d" % bi], writes=["qT%d" % c])
                    for tl in range(4):
                        t = tgp * 4 + tl
                        sc_ = scs[t % 2]
                        sck = "scs%d" % (t % 2)
                        okeys = []
                        for cg in range(4):
                            bi = 4 + cg % 2
                            acc = PSF[bi]
                            for ci in range(4):
                                c = cg * 4 + ci
                                T.op("pe", lambda e, acc=acc, ci=ci, c=c, tl=tl: e.matmul(acc[:, ci * 128:(ci + 1) * 128], lhsT=qT[:, c, tl * 128:(tl + 1) * 128],
                                                                                         rhs=skT[:, c, :], start=True, stop=True),
                                     reads=["qT%d" % c, "skT0", "skT1"], writes=["psf%d" % bi])
                            if cg % 2 == 0:
                                T.op("act", lambda e, acc=acc, cg=cg, sc_=sc_: e.activation(out=sc_[:, cg * 512:(cg + 1) * 512], in_=acc[:, 0:512], func=AF.Copy),
                                     reads=["psf%d" % bi], writes=[sck + "_%d" % cg])
                            else:
                                T.op("dve", lambda e, acc=acc, cg=cg, sc_=sc_: e.tensor_copy(out=sc_[:, cg * 512:(cg + 1) * 512], in_=acc[:, 0:512]),
                                     reads=["psf%d" % bi], writes=[sck + "_%d" % cg])
                            okeys.append(sck + "_%d" % cg)
                        T.dma("sp", lambda e, t=t, sc_=sc_: e.dma_start(out=scr_d[t * 128:(t + 1) * 128, :], in_=sc_[:]), "scso%d" % (t % 2), reads=okeys)
                T.barrier()
            if not do_E:
                return
            with ExitStack() as es:
                def sb(name, shape, dt=F32):
                    return es.enter_context(nc.sbuf_tensor(name, shape, dt))
                W1 = sb("W1", [128, D])
                W2 = sb("W2", [128, D])
                W3 = sb("W3", [128, D])
                W4 = sb("W4", [128, D])
                v16 = sb("v16", [128, 16, 16])
                i16 = sb("i16", [128, 16, 16], U32)
                i16f = sb("i16f", [128, 16, 16])
                top = sb("topD", [128, 8, 16])
                pos = sb("posD", [128, 8, 16], U32)
                posf = sb("posfD", [128, 8, 16])
                af_ = sb("afD", [128, 8, 16])
                bf_ = sb("bfD", [128, 8, 16])
                s1 = sb("s1D", [128, 8, 16])
                s2 = sb("s2D", [128, 8, 16])
                eg = sb("egD", [128, 8, 16])
                esum = sb("esumD", [128, 8])
                io16i = sb("io16i", [128, 16], I32)
                io16 = sb("io16", [128, 16])
                thr15 = sb("thr15", [128, 15])
                NS = 16
                rows = [sb("rowE%d" % i, [128, D], BF16) for i in range(NS)]
                ND = 4
                diag = [sb("diagE%d" % i, [128, 128], BF16) for i in range(ND)]
                gf = sb("gfE", [128, D])
                hnb2 = [sb("hnbE%d" % i, [128, D], BF16) for i in range(2)]
                yacc = sb("yaccE", [128, D])
                dots = sb("dotsE", [128, 128])
                wgt = sb("wgtE", [128, 128])
                junkf = sb("junkfE", [128, D], BF16)
                junka = sb("junkaE", [128, D], BF16)
                NP = 4
                prod = [sb("prodE%d" % i, [128, D], BF16) for i in range(NP)]
                T.dma("sp", lambda e: e.dma_start(out=gf[:], in_=gf_d[:, :]), "gf", writes=["gf"])
                T.op("pool", lambda e: e.iota(io16i[:], [[1, 16]], base=0, channel_multiplier=0), writes=["io16i"])
                T.op("dve", lambda e: e.tensor_copy(out=io16[:], in_=io16i[:]), reads=["io16i"], writes=["io16"])
                T.op("dve", lambda e: e.tensor_scalar(out=thr15[:], in0=io16[:, 0:15], scalar1=16.0, scalar2=16.0, op0=ALU.mult, op1=ALU.add),
                     reads=["io16"], writes=["thr15"])

                def topk(t):
                    sl = t % 3
                    T.dma("sp", lambda e: e.dma_start(out=W1[:], in_=scr_d[t * 128:(t + 1) * 128, :]), "scl", writes=["W1"])
                    sc = W1[:].rearrange("p (c n) -> p c n", c=16)
                    sc2 = W2[:].rearrange("p (c n) -> p c n", c=16)
                    for c in range(16):
                        T.op("dve", lambda e, c=c: e.max(out=v16[:, c, 0:8], in_=sc[:, c, :]), reads=["W1"], writes=["v16"])
                        T.op("dve", lambda e, c=c: e.max_index(out=i16[:, c, 0:8], in_max=v16[:, c, 0:8], in_values=sc[:, c, :]), reads=["W1", "v16"], writes=["i16"])
                        T.op("dve", lambda e, c=c: e.match_replace(out=sc2[:, c, :], in_to_replace=v16[:, c, 0:8], in_values=sc[:, c, :], imm_value=-1e30),
                             reads=["W1", "v16"], writes=["W2"])
                        T.op("dve", lambda e, c=c: e.max(out=v16[:, c, 8:16], in_=sc2[:, c, :]), reads=["W2", "v16"], writes=["v16"])
                        T.op("dve", lambda e, c=c: e.max_index(out=i16[:, c, 8:16], in_max=v16[:, c, 8:16], in_values=sc2[:, c, :]), reads=["W2", "v16", "i16"], writes=["i16"])
                    v4 = v16[:].rearrange("p (h s) k -> p h s k", s=2)
                    cand = W3[:].rearrange("p (h a b) -> p h a b", h=8, a=16)
                    T.op("dve", lambda e: e.tensor_tensor(out=cand, in0=v4[:, :, 0, :].unsqueeze(3).to_broadcast([128, 8, 16, 16]),
                                                          in1=v4[:, :, 1, :].unsqueeze(2).to_broadcast([128, 8, 16, 16]), op=ALU.add),
                         reads=["v16"], writes=["W3"])
                    c3 = W3[:].rearrange("p (h n) -> p h n", h=8)
                    c4 = W4[:].rearrange("p (h n) -> p h n", h=8)
                    for h in range(8):
                        T.op("dve", lambda e, h=h: e.max(out=top[:, h, 0:8], in_=c3[:, h, :]), reads=["W3"], writes=["top"])
                        T.op("dve", lambda e, h=h: e.max_index(out=pos[:, h, 0:8], in_max=top[:, h, 0:8], in_values=c3[:, h, :]), reads=["W3", "top"], writes=["pos"])
                        T.op("dve", lambda e, h=h: e.match_replace(out=c4[:, h, :], in_to_replace=top[:, h, 0:8], in_values=c3[:, h, :], imm_value=-1e30),
                             reads=["W3", "top"], writes=["W4"])
                        T.op("dve", lambda e, h=h: e.max(out=top[:, h, 8:16], in_=c4[:, h, :]), reads=["W4", "top"], writes=["top"])
                        T.op("dve", lambda e, h=h: e.max_index(out=pos[:, h, 8:16], in_max=top[:, h, 8:16], in_values=c4[:, h, :]), reads=["W4", "top", "pos"], writes=["pos"])
                    T.op("dve", lambda e: e.tensor_copy(out=posf[:], in_=pos[:]), reads=["pos"], writes=["posf"])
                    T.op("dve", lambda e: e.tensor_copy(out=i16f[:], in_=i16[:]), reads=["i16"], writes=["i16f"])
                    big = W1[:].rearrange("p (h k a) -> p h k a", h=8, k=16)
                    big2 = W2[:].rearrange("p (h k a) -> p h k a", h=8, k=16)
                    T.op("dve", lambda e: e.tensor_tensor(out=big[:, :, :, 0:15], in0=posf[:].unsqueeze(3).to_broadcast([128, 8, 16, 15]),
                                                          in1=thr15[:].unsqueeze(1).unsqueeze(1).to_broadcast([128, 8, 16, 15]), op=ALU.is_ge),
                         reads=["posf", "thr15", "W1"], writes=["W1"])
                    T.op("dve", lambda e: e.tensor_reduce(out=af_[:], in_=big[:, :, :, 0:15], axis=AX.X, op=ALU.add), reads=["W1"], writes=["af"])
                    T.op("dve", lambda e: e.scalar_tensor_tensor(out=bf_[:], in0=af_[:], scalar=-16.0, in1=posf[:], op0=ALU.mult, op1=ALU.add),
                         reads=["af", "posf"], writes=["bf"])
                    i4 = i16f[:].rearrange("p (h s) k -> p h s k", s=2)
                    for (src, side, dst, dk) in ((af_, 0, s1, "s1"), (bf_, 1, s2, "s2")):
                        sk_ = "af" if side == 0 else "bf"
                        T.op("dve", lambda e, src=src: e.tensor_tensor(out=big, in0=src[:].unsqueeze(3).to_broadcast([128, 8, 16, 16]),
                                                                       in1=io16[:].unsqueeze(1).unsqueeze(1).to_broadcast([128, 8, 16, 16]), op=ALU.is_equal),
                             reads=[sk_, "io16", "W1"], writes=["W1"])
                        T.op("dve", lambda e, side=side: e.tensor_tensor(out=big2, in0=big, in1=i4[:, :, side, :].unsqueeze(2).to_broadcast([128, 8, 16, 16]), op=ALU.mult),
                             reads=["W1", "i16f", "W2"], writes=["W2"])
                        T.op("dve", lambda e, dst=dst: e.tensor_reduce(out=dst[:], in_=big2, axis=AX.X, op=ALU.add), reads=["W2"], writes=[dk])
                    T.op("dve", lambda e: e.scalar_tensor_tensor(out=s1[:], in0=s1[:], scalar=128.0, in1=s2[:], op0=ALU.mult, op1=ALU.add),
                         reads=["s1", "s2"], writes=["s1"])
                    T.op("dve", lambda e: e.tensor_copy(out=eid2[:, sl, :], in_=s1[:].rearrange("p h k -> p (h k)")), reads=["s1"], writes=["eid%d" % sl])
                    T.op("dve", lambda e: e.tensor_tensor(out=eg[:], in0=top[:], in1=top[:, :, 0:1].to_broadcast([128, 8, 16]), op=ALU.subtract),
                         reads=["top"], writes=["eg"])
                    T.op("act", lambda e: e.activation(out=eg[:], in_=eg[:], func=AF.Exp), reads=["eg"], writes=["eg"])
                    T.op("dve", lambda e: e.tensor_reduce(out=esum[:], in_=eg[:], axis=AX.X, op=ALU.add), reads=["eg"], writes=["esum"])
                    T.op("dve", lambda e: e.reciprocal(out=esum[:], in_=esum[:]), reads=["esum"], writes=["esum"])
                    T.op("dve", lambda e: e.tensor_tensor(out=gat2[:, sl, :].rearrange("p (h k) -> p h k", h=8), in0=eg[:],
                                                          in1=esum[:].unsqueeze(2).to_broadcast([128, 8, 16]), op=ALU.mult),
                         reads=["eg", "esum"], writes=["gat%d" % sl])
                    if debug:
                        T.dma("sp", lambda e: e.dma_start(out=dbg_d[t, :, 0:128], in_=eid2[:, sl, :].bitcast(F32)), "dbg0", reads=["eid%d" % sl])
                        T.dma("sp", lambda e: e.dma_start(out=dbg_d[t, :, 128:256], in_=gat2[:, sl, :]), "dbg1", reads=["gat%d" % sl])

                dkeys = ["dots%d" % i for i in range(128)]
                ykeys = ["yacc%d" % ng for ng in range(4)]
                x1s = [x1i, x1j, x1k]
                cnt = {"gi": 0, "di": 0, "pi": 0}

                def topk_ops(t):
                    saved = (T.op, T.dma)
                    lst = []
                    T.op = lambda *a, **k: lst.append(("op", a, k))
                    T.dma = lambda *a, **k: lst.append(("dma", a, k))
                    try:
                        topk(t)
                    finally:
                        T.op, T.dma = saved
                    return lst

                def play(lst, n):
                    for _ in range(n):
                        if not lst:
                            return
                        kind, a, k = lst.pop(0)
                        (T.op if kind == "op" else T.dma)(*a, **k)

                def load_x1(t):
                    xs = x1s[t % 3]
                    T.dma("sp", lambda e: e.dma_start(out=xs[:], in_=x1_d[t * 128:(t + 1) * 128, :]), "x1s%d" % (t % 3), writes=["x1s%d" % (t % 3)])

                def make_hnb(t):
                    xs = x1s[t % 3]
                    hb_ = hnb2[t % 2]
                    T.op("dve", lambda e: e.scalar_tensor_tensor(out=hb_[:], in0=xs[:], scalar=rs[:, t:t + 1], in1=g2[:], op0=ALU.mult, op1=ALU.mult),
                         reads=["x1s%d" % (t % 3), "g2"], writes=["hnb%d" % (t % 2)])

                def next_row():
                    s_ = cnt["gi"] % NS
                    cnt["gi"] += 1
                    return rows[s_], "row%d" % s_

                def u_row(t, hk):
                    sl = t % 3
                    rw, rk = next_row()
                    T.dma("pool", lambda e: e.indirect_dma_start(
                        out=rw[:], out_offset=None, in_=eub_d[:, :], in_offset=bass.IndirectOffsetOnAxis(ap=eid2[:, sl, hk:hk + 1], axis=0)),
                          rk, reads=["eid%d" % sl], writes=[rk])
                    pr = prod[cnt["pi"] % NP]
                    pk = "prod%d" % (cnt["pi"] % NP)
                    cnt["pi"] += 1
                    hb_ = hnb2[t % 2]
                    T.op("dve", lambda e: e.tensor_tensor(out=pr[:], in0=rw[:], in1=hb_[:], op=ALU.mult), reads=[rk, "hnb%d" % (t % 2)], writes=[pk])
                    if hk % 5 != 4:
                        T.op("act", lambda e: e.activation(out=junka[:], in_=pr[:], func=AF.Copy, accum_out=dots[:, hk:hk + 1]),
                             reads=[pk, "dotsz"], writes=["dots%d" % hk])
                    else:
                        T.op("dve", lambda e: e.tensor_scalar(out=junkf[:], in0=pr[:], scalar1=1.0, scalar2=0.0, op0=ALU.mult, op1=ALU.add,
                                                              accum_out=dots[:, hk:hk + 1]),
                             reads=[pk, "dotsz"], writes=["dots%d" % hk])

                def v_row(t, hk):
                    sl = t % 3
                    rw, rk = next_row()
                    dg = diag[cnt["di"] % ND]
                    dgk = "diag%d" % (cnt["di"] % ND)
                    cnt["di"] += 1
                    T.dma("pool", lambda e: e.indirect_dma_start(
                        out=rw[:], out_offset=None, in_=evb_d[:, :], in_offset=bass.IndirectOffsetOnAxis(ap=eid2[:, sl, hk:hk + 1], axis=0)),
                          rk, reads=["eid%d" % sl], writes=[rk])
                    T.op("act", lambda e: e.activation(out=dg[:], in_=ident[:], func=AF.Copy, scale=wgt[:, hk:hk + 1]), reads=["wgt"], writes=[dgk])
                    for ng in range(4):
                        T.op("pe", lambda e, ng=ng: e.matmul(PSF[ng][:, 0:512], lhsT=dg[:], rhs=rw[:, ng * 512:(ng + 1) * 512],
                                                            start=(hk == 0), stop=(hk == 127)),
                             reads=[dgk, rk], writes=["psf%d" % ng])

                def make_wgt(t):
                    sl = t % 3
                    T.op("act", lambda e: e.activation(out=wgt[:], in_=dots[:], func=AF.Gelu), reads=dkeys, writes=["wgt"])
                    T.op("dve", lambda e: e.tensor_tensor(out=wgt[:], in0=wgt[:], in1=gat2[:, sl, :], op=ALU.mult), reads=["wgt", "gat%d" % sl], writes=["wgt"])

                def finish(t):
                    xs = x1s[t % 3]
                    xk = "x1s%d" % (t % 3)
                    for ng in range(4):
                        T.op("dve", lambda e, ng=ng: e.tensor_tensor(out=yacc[:, ng * 512:(ng + 1) * 512], in0=PSF[ng][:, 0:512], in1=xs[:, ng * 512:(ng + 1) * 512], op=ALU.add),
                             reads=["psf%d" % ng, xk], writes=["yacc%d" % ng])
                    c2 = NT_OWN + t
                    T.op("act", lambda e: e.activation(out=junk[:], in_=yacc[:], func=AF.Square, accum_out=ssq[:, c2:c2 + 1]),
                         reads=ykeys + ["ssqDz"], writes=["ssqE%d" % t])
                    T.op("act", lambda e: e.activation(out=rs[:, c2:c2 + 1], in_=ssq[:, c2:c2 + 1], func=AF.Sqrt, scale=1.0 / D, bias=epsc[:, 0:1]),
                         reads=["ssqE%d" % t], writes=["rsE%d" % t])
                    T.op("dve", lambda e: e.reciprocal(out=rs[:, c2:c2 + 1], in_=rs[:, c2:c2 + 1]), reads=["rsE%d" % t], writes=["rsE%d" % t])
                    T.op("dve", lambda e: e.scalar_tensor_tensor(out=yacc[:], in0=yacc[:], scalar=rs[:, c2:c2 + 1], in1=gf[:], op0=ALU.mult, op1=ALU.mult),
                         reads=ykeys + ["rsE%d" % t, "gf"], writes=ykeys)
                    T.dma("sp", lambda e: e.dma_start(out=y_d[t * 128:(t + 1) * 128, :], in_=yacc[:]), "yout", reads=ykeys)

                def zero_dots():
                    T.op("pool", lambda e: e.memset(dots[:], 0.0), writes=["dotsz"] + dkeys)

                load_x1(0)
                load_x1(1)
                load_x1(2)
                play(topk_ops(0), 10 ** 6)
                play(topk_ops(1), 10 ** 6)
                make_hnb(0)
                zero_dots()
                for hk in range(128):
                    u_row(0, hk)
                make_wgt(0)
                make_hnb(1)
                for t in range(NT_OWN):
                    pend = topk_ops(t + 2) if t + 2 < NT_OWN else []
                    if t + 1 < NT_OWN:
                        zero_dots()
                    for hk in range(128):
                        v_row(t, hk)
                        if t + 1 < NT_OWN:
                            u_row(t + 1, hk)
                        play(pend, 2)
                        if hk == 64 and t + 2 < NT_OWN:
                            make_hnb(t + 2)
                    play(pend, 10 ** 6)
                    if t + 1 < NT_OWN:
                        make_wgt(t + 1)
                    finish(t)
                    if t + 3 < NT_OWN:
                        load_x1(t + 3)
                T.barrier()

    if "A" in phases:
        phase_A()
    if "B" in phases:
        phase_B()
    if "C" in phases:
        phase_C()
    if "D" in phases:
        phase_DE(do_E=("E" in phases))
    T.barrier()
    gs.close()
    return nc, T


def make_in_maps(x_prompt, x_sample, norm1_g, w_in, a_out_g, b_out_g, sink, w_out, norm2_g, w_pq,
                 sub_keys, expert_u, expert_v, normf_g):
    f = np.float32
    xs = np.concatenate([np.asarray(x_prompt, f).reshape(-1, D), np.asarray(x_sample, f).reshape(-1, D)], axis=0)
    ntot = xs.shape[0]
    assert ntot == NCORES * T_OWN
    xpad = np.zeros((ntot + 2 * HALO, D), f)
    xpad[HALO:HALO + ntot] = xs
    nprompt = np.asarray(x_prompt).shape[0] * np.asarray(x_prompt).shape[1]
    seqlen_p = np.asarray(x_prompt).shape[1]
    seg = np.full((ntot + 2 * HALO,), 6.0, f)
    ids = np.arange(ntot)
    seg[HALO:HALO + ntot] = np.where(ids < nprompt, ids // seqlen_p, nprompt // seqlen_p).astype(f)
    segq = np.stack([seg * seg, seg, np.ones_like(seg)]).astype(f)
    segk = np.stack([-SEG_B * np.ones_like(seg), 2 * SEG_B * seg, -SEG_B * seg * seg]).astype(f)
    rep = lambda v: np.ascontiguousarray(np.broadcast_to(np.asarray(v, f).reshape(1, D), (128, D)))
    gcat = np.concatenate([np.asarray(a_out_g, f).reshape(-1), np.asarray(b_out_g, f).reshape(-1)])
    gab = np.ascontiguousarray(gcat.reshape(16, 128).T)
    sk = np.asarray(sink, f).reshape(-1)
    sinkc = np.ascontiguousarray(np.broadcast_to(sk.reshape(1, 16), (128, 16)))
    shared = {
        "g1b": rep(norm1_g), "g2b": rep(norm2_g), "gfb": rep(normf_g), "gab": gab, "sinkc": sinkc,
        "w_in": np.ascontiguousarray(np.asarray(w_in, f).reshape(D, IN_W)),
        "w_out": np.ascontiguousarray(np.asarray(w_out, f).reshape(D, D)),
        "w_pq": np.ascontiguousarray(np.asarray(w_pq, f).reshape(D, D)),
        "sub_keys": np.ascontiguousarray(np.asarray(sub_keys, f).reshape(16 * 128, 128)),
        "expert_u": np.ascontiguousarray(np.asarray(expert_u, f).reshape(16384, D)),
        "expert_v": np.ascontiguousarray(np.asarray(expert_v, f).reshape(16384, D)),
    }
    in_maps = []
    for c in range(NCORES):
        lo = c * T_OWN
        m = dict(shared)
        m["x_ext"] = np.ascontiguousarray(xpad[lo:lo + T_EXT])
        m["segq"] = np.ascontiguousarray(segq[:, lo:lo + T_EXT])
        m["segk"] = np.ascontiguousarray(segk[:, lo:lo + T_EXT])
        in_maps.append(m)
    return in_maps


def kernel(x_prompt, x_sample, norm1_g, w_in, a_out_g, b_out_g, sink, w_out, norm2_g, w_pq,
           sub_keys, expert_u, expert_v, normf_g):
    in_maps = make_in_maps(x_prompt, x_sample, norm1_g, w_in, a_out_g, b_out_g, sink, w_out, norm2_g, w_pq,
                           sub_keys, expert_u, expert_v, normf_g)
    nc, _ = build_program()
    res = run_bass_kernel_spmd(nc, in_maps, core_ids=list(range(NCORES)))
    y = np.concatenate([np.asarray(r["y"], np.float32) for r in res.results], axis=0)
    xp = np.asarray(x_prompt)
    xsm = np.asarray(x_sample)
    npr = xp.shape[0] * xp.shape[1]
    return (np.ascontiguousarray(y[:npr]).reshape(xp.shape), np.ascontiguousarray(y[npr:]).reshape(xsm.shape))
```

```python
import math
from contextlib import ExitStack

import numpy as np
import concourse.bass as bass
import concourse.mybir as mybir
from concourse.bass_utils import run_bass_kernel_spmd

F32 = mybir.dt.float32
BF16 = mybir.dt.bfloat16
I32 = mybir.dt.int32
U32 = mybir.dt.uint32
AF = mybir.ActivationFunctionType
ALU = mybir.AluOpType
AX = mybir.AxisListType

NCORES = 8
D = 2048
KC = 16
T_OWN = 3072
HALO = 1024
T_EXT = T_OWN + 2 * HALO
NT_OWN = T_OWN // 128
NT_EXT = T_EXT // 128
SBT = T_EXT // 2
IN_W = 4608
NOC = IN_W // 128
EPS = 1e-6
SEG_B = float(2 ** 17)
NEGM = -30000.0


class _Op:
    __slots__ = ("eng", "fn", "is_dma", "semkey", "deps", "needed", "token", "emitted", "idx")

    def __init__(self, eng, fn, is_dma, semkey):
        self.eng = eng
        self.fn = fn
        self.is_dma = is_dma
        self.semkey = semkey
        self.deps = []
        self.needed = False
        self.token = None
        self.emitted = False
        self.idx = 0


class Tracker:
    ENGS = ("pe", "act", "dve", "pool", "sp")

    def __init__(self, nc):
        self.nc = nc
        self.eng = {"pe": nc.tensor, "act": nc.scalar, "dve": nc.vector,
                    "pool": nc.gpsimd, "sp": nc.sync}
        self.esem = {e: nc.semaphore("s_" + e).__enter__() for e in self.ENGS}
        self.ecount = {e: 0 for e in self.ENGS}
        self.dsem = {}
        self.dlast = {}
        self.pending = []
        self.last_w = {}
        self.readers = {}
        self.known = {e: {} for e in self.ENGS}
        self.last_op = {e: None for e in self.ENGS}
        self.n_ops = 0
        self.n_waits = 0
        self.eidx = {e: 0 for e in self.ENGS}
        self.embed = True

    def _record(self, op, reads, writes):
        if not op.is_dma:
            self.eidx[op.eng] += 1
            op.idx = self.eidx[op.eng]
        deps = []
        for k in reads:
            w = self.last_w.get(k)
            if w is not None:
                deps.append(w)
        for k in writes:
            w = self.last_w.get(k)
            if w is not None:
                deps.append(w)
            deps.extend(self.readers.get(k, ()))
        if op.is_dma:
            prev = self.dlast.get(op.semkey)
            if prev is not None:
                deps.append(prev)
            self.dlast[op.semkey] = op
        seen = set()
        for d in deps:
            if d is op or id(d) in seen:
                continue
            seen.add(id(d))
            if (not d.is_dma) and (not op.is_dma) and d.eng == "pe" and op.eng == "pe":
                continue
            if (not d.is_dma) and (not op.is_dma) and d.eng == op.eng and op.eng in ("dve", "act") \
                    and op.idx - d.idx >= 3:
                continue
            d.needed = True
            op.deps.append(d)
        for k in writes:
            self.last_w[k] = op
            self.readers[k] = []
        for k in reads:
            if k in writes:
                continue
            lst = self.readers.setdefault(k, [])
            if not op.is_dma:
                lst[:] = [r for r in lst if r.is_dma or r.eng != op.eng]
            lst.append(op)
        self.pending.append(op)
        if not op.is_dma:
            self.last_op[op.eng] = op
        self.n_ops += 1
        return op

    def op(self, eng, fn, reads=(), writes=()):
        return self._record(_Op(eng, fn, False, None), tuple(reads), tuple(writes))

    def dma(self, eng, fn, semkey, reads=(), writes=()):
        return self._record(_Op(eng, fn, True, semkey), tuple(reads), tuple(writes))

    def _wait(self, engname, sem, cnt):
        kn = self.known[engname]
        key = id(sem)
        if kn.get(key, 0) >= cnt:
            return
        self.eng[engname].wait_ge(sem, cnt)
        kn[key] = cnt
        self.n_waits += 1

    def flush(self):
        for e in self.ENGS:
            if self.last_op[e] is not None and not self.last_op[e].emitted:
                self.last_op[e].needed = True
        for op in self.pending:
            need = {}
            for d in op.deps:
                assert d.emitted and d.token is not None, "dep without token"
                sem, cnt = d.token
                if need.get(id(sem), (None, 0))[1] < cnt:
                    need[id(sem)] = (sem, cnt)
            kn = self.known[op.eng]
            todo = [(sem, cnt) for sem, cnt in need.values() if kn.get(id(sem), 0) < cnt]
            emb = None
            if todo and self.embed and (not op.is_dma) and op.eng in ("pe", "act", "dve"):
                emb = todo.pop()
            for sem, cnt in todo:
                self._wait(op.eng, sem, cnt)
            ins = op.fn(self.eng[op.eng])
            if emb is not None:
                ins._wait_ge(emb[0], emb[1])
                kn[id(emb[0])] = emb[1]
                self.n_waits += 1
            if op.is_dma:
                ent = self.dsem.get(op.semkey)
                if ent is None:
                    ent = [self.nc.semaphore("d_%d" % len(self.dsem)).__enter__(), 0]
                    self.dsem[op.semkey] = ent
                ent[1] += 16
                ins.then_inc(ent[0], 16)
                op.token = (ent[0], ent[1])
            elif op.needed:
                self.ecount[op.eng] += 1
                ins.then_inc(self.esem[op.eng], 1)
                op.token = (self.esem[op.eng], self.ecount[op.eng])
            op.emitted = True
            op.fn = None
        self.pending = []

    def barrier(self):
        self.flush()
        for e in self.ENGS:
            for e2 in self.ENGS:
                if self.ecount[e2] > 0:
                    self._wait(e, self.esem[e2], self.ecount[e2])
            for sem, cnt in self.dsem.values():
                if cnt > 0:
                    self._wait(e, sem, cnt)
        self.last_w = {}
        self.readers = {}
        self.dlast = {}
        self.last_op = {e: None for e in self.ENGS}


def cs(start, n, step):
    return slice(start, start + (n - 1) * step + 1, step)


def alibi_slopes():
    s = [2.0 ** (-8.0 * (i + 1) / 32.0) for i in range(32)]
    return s[0::2], s[1::2]


def build_program(debug=False, phases="ABCDE"):
    nc = bass.Bass("TRN2", target_bir_lowering=False)
    T = Tracker(nc)
    skind = "ExternalOutput" if debug else "Internal"

    def din(name, shape, dt=F32):
        return nc.dram_tensor(name, shape, dt, kind="ExternalInput").ap()

    x_ext = din("x_ext", [T_EXT, D])
    segq_d = din("segq", [3, T_EXT])
    segk_d = din("segk", [3, T_EXT])
    g1_d = din("g1b", [128, D])
    g2_d = din("g2b", [128, D])
    gf_d = din("gfb", [128, D])
    gab_d = din("gab", [128, 16])
    sink_d = din("sinkc", [128, 16])
    w_in_d = din("w_in", [D, IN_W])
    w_out_d = din("w_out", [D, D])
    w_pq_d = din("w_pq", [D, D])
    sk_d = din("sub_keys", [16 * 128, 128])
    eu_d = din("expert_u", [16384, D])
    ev_d = din("expert_v", [16384, D])
    y_d = nc.dram_tensor("y", [T_OWN, D], F32, kind="ExternalOutput").ap()
    qkvT_d = nc.dram_tensor("qkvT", [IN_W, T_EXT], BF16, kind=skind).ap()
    oaT_d = nc.dram_tensor("oaT", [D, T_OWN], F32, kind=skind).ap()
    rst_d = nc.dram_tensor("rst", [2, 128, T_OWN], F32, kind=skind).ap()
    x1_d = nc.dram_tensor("x1", [T_OWN, D], F32, kind=skind).ap()
    scr_d = nc.dram_tensor("scr", [T_OWN, D], F32, kind="Internal").ap()
    eub_d = nc.dram_tensor("eub", [16384, D], BF16, kind="Internal").ap()
    evb_d = nc.dram_tensor("evb", [16384, D], BF16, kind="Internal").ap()
    if debug:
        dbg_d = nc.dram_tensor("dbg", [NT_OWN, 128, 512], F32, kind="ExternalOutput").ap()

    gs = ExitStack()

    def gsb(name, shape, dt=F32):
        return gs.enter_context(nc.sbuf_tensor(name, shape, dt))

    PSB = [gs.enter_context(nc.psum_tensor("psb%d" % i, [128, 1024], BF16)) for i in range(2)]
    PSF = [gs.enter_context(nc.psum_tensor("psf%d" % i, [128, 512], F32)) for i in range(6)]

    ident = gsb("ident", [128, 128], BF16)
    ones128 = gsb("ones128", [128, 128], BF16)
    onesA = gsb("onesA", [128, 128], BF16)
    onesB = gsb("onesB", [128, 128], BF16)
    epsc = gsb("epsc", [128, 1], F32)
    with ExitStack() as es:
        idi = es.enter_context(nc.sbuf_tensor("idi", [128, 128], I32))
        idf = es.enter_context(nc.sbuf_tensor("idf", [128, 128], F32))
        T.op("pool", lambda e: e.iota(idi[:], [[1, 128]], base=0, channel_multiplier=-1), writes=["idi"])
        T.op("dve", lambda e: e.tensor_copy(out=idf[:], in_=idi[:]), reads=["idi"], writes=["idf"])
        T.op("dve", lambda e: e.tensor_single_scalar(out=ident[:], in_=idf[:], scalar=0.0, op=ALU.is_equal),
             reads=["idf"], writes=["ident"])
        T.op("pool", lambda e: e.memset(ones128[:], 1.0), writes=["ones128"])
        T.op("pool", lambda e: e.memset(onesA[:], 0.0), writes=["onesA"])
        T.op("pool", lambda e: e.memset(onesA[:, 0:64], 1.0), reads=["onesA"], writes=["onesA"])
        T.op("pool", lambda e: e.memset(onesB[:], 0.0), writes=["onesB"])
        T.op("pool", lambda e: e.memset(onesB[:, 64:128], 1.0), reads=["onesB"], writes=["onesB"])
        T.op("pool", lambda e: e.memset(epsc[:], EPS), writes=["epsc"])
        T.barrier()

    def rms_scale(xi, xkey, ssq_col, rs_col, key, nfeat, junk):
        T.op("act", lambda e: e.activation(out=junk[:], in_=xi[:], func=AF.Square, accum_out=ssq_col),
             reads=[xkey, key + "z"], writes=[key + "s"])
        T.op("act", lambda e: e.activation(out=rs_col, in_=ssq_col, func=AF.Sqrt, scale=1.0 / nfeat, bias=epsc[:, 0:1]),
             reads=[key + "s"], writes=[key + "r"])
        T.op("dve", lambda e: e.reciprocal(out=rs_col, in_=rs_col), reads=[key + "r"], writes=[key + "r"])

    def transpose16(src, srckey, dst_fn, dstkey):
        for half in range(2):
            pb = PSB[half]
            for k8 in range(8):
                k = half * 8 + k8
                T.op("pe", lambda e, pb=pb, k8=k8, k=k: e.transpose(pb[:, k8 * 128:(k8 + 1) * 128],
                                                                   in_=src[:, k * 128:(k + 1) * 128], identity=ident[:]),
                     reads=[srckey], writes=["psb%d" % half])
            dst = dst_fn(half)
            if half == 0:
                T.op("act", lambda e, pb=pb, dst=dst: e.activation(out=dst, in_=pb[:].rearrange("p (k t) -> p k t", k=8), func=AF.Copy),
                     reads=["psb%d" % half], writes=[dstkey])
            else:
                T.op("dve", lambda e, pb=pb, dst=dst: e.tensor_copy(out=dst, in_=pb[:].rearrange("p (k t) -> p k t", k=8)),
                     reads=["psb%d" % half], writes=[dstkey])

    def phase_A():
        with ExitStack() as es:
            def sb(name, shape, dt=F32):
                return es.enter_context(nc.sbuf_tensor(name, shape, dt))
            xnT = sb("xnT", [128, KC, SBT], BF16)
            xin = [sb("xinA%d" % i, [128, D], F32) for i in range(2)]
            xnb = [sb("xnbA%d" % i, [128, D], BF16) for i in range(2)]
            junk = sb("junkA", [128, D], BF16)
            ssq = sb("ssqA1", [128, NT_EXT])
            rstd = sb("rstdA1", [128, NT_EXT])
            wb = [sb("wbA%d" % i, [128, KC, 128], BF16) for i in range(3)]
            stg = [sb("stgA%d" % i, [128, SBT], BF16) for i in range(2)]
            g1 = sb("g1A", [128, D])
            T.dma("sp", lambda e: e.dma_start(out=g1[:], in_=g1_d[:, :]), "ldg", writes=["g1"])
            T.op("pool", lambda e: e.memset(ssq[:], 0.0), writes=["ssqz"])
            win = w_in_d.rearrange("(kc p) c -> p kc c", p=128)
            for sbi in range(2):
                for tl in range(SBT // 128):
                    t = sbi * (SBT // 128) + tl
                    xi = xin[t % 2]
                    xb = xnb[t % 2]
                    xk = "xin%d" % (t % 2)
                    bk = "xnb%d" % (t % 2)
                    T.dma("sp", lambda e, xi=xi, t=t: e.dma_start(out=xi[:], in_=x_ext[t * 128:(t + 1) * 128, :]),
                          xk, writes=[xk])
                    T.op("act", lambda e, xi=xi, t=t: e.activation(out=junk[:], in_=xi[:], func=AF.Square, accum_out=ssq[:, t:t + 1]),
                         reads=[xk, "ssqz"], writes=["ssq%d" % t])
                    T.op("act", lambda e, t=t: e.activation(out=rstd[:, t:t + 1], in_=ssq[:, t:t + 1], func=AF.Sqrt,
                                                            scale=1.0 / D, bias=epsc[:, 0:1]),
                         reads=["ssq%d" % t], writes=["rstd%d" % t])
                    T.op("dve", lambda e, t=t: e.reciprocal(out=rstd[:, t:t + 1], in_=rstd[:, t:t + 1]),
                         reads=["rstd%d" % t], writes=["rstd%d" % t])
                    T.op("dve", lambda e, xi=xi, xb=xb, t=t: e.scalar_tensor_tensor(out=xb[:], in0=xi[:], scalar=rstd[:, t:t + 1], in1=g1[:],
                                                                                    op0=ALU.mult, op1=ALU.mult),
                         reads=[xk, "rstd%d" % t, "g1"], writes=[bk])
                    transpose16(xb, bk, lambda half, tl=tl: xnT[:, half * 8:half * 8 + 8, tl * 128:(tl + 1) * 128], "xnT%d" % tl)
                for oc in range(NOC):
                    w = wb[oc % 3]
                    wk = "wb%d" % (oc % 3)
                    T.dma("pool", lambda e, w=w, oc=oc: e.dma_start(out=w[:], in_=win[:, :, oc * 128:(oc + 1) * 128]), wk, writes=[wk])
                    st = stg[oc % 2]
                    skeys = []
                    is_q = (oc < 8) or (24 <= oc < 32)
                    tgs = list(range(SBT // 512))
                    if is_q:
                        tgs = [2, 3, 4] if sbi == 0 else [0, 1, 2]
                    for tg in tgs:
                        bi = (oc * (SBT // 512) + tg) % 6
                        acc = PSF[bi]
                        xkeys = ["xnT%d" % (tg * 4 + i) for i in range(4)]
                        for kc in range(KC):
                            T.op("pe", lambda e, acc=acc, w=w, kc=kc, tg=tg: e.matmul(acc[:, 0:512], lhsT=w[:, kc, :],
                                                                                   rhs=xnT[:, kc, tg * 512:(tg + 1) * 512],
                                                                                   start=(kc == 0), stop=(kc == KC - 1)),
                                 reads=[wk] + xkeys, writes=["psf%d" % bi])
                        sk_ = "stg%d_%d" % (oc % 2, tg)
                        skeys.append(sk_)
                        if tg % 2 == 0:
                            T.op("act", lambda e, acc=acc, st=st, tg=tg: e.activation(out=st[:, tg * 512:(tg + 1) * 512], in_=acc[:, 0:512], func=AF.Copy),
                                 reads=["psf%d" % bi], writes=[sk_])
                        else:
                            T.op("dve", lambda e, acc=acc, st=st, tg=tg: e.tensor_copy(out=st[:, tg * 512:(tg + 1) * 512], in_=acc[:, 0:512]),
                                 reads=["psf%d" % bi], writes=[sk_])
                    c_lo, c_hi = tgs[0] * 512, (tgs[-1] + 1) * 512
                    T.dma("sp", lambda e, st=st, oc=oc, sbi=sbi, c_lo=c_lo, c_hi=c_hi: e.dma_start(
                        out=qkvT_d[oc * 128:(oc + 1) * 128, sbi * SBT + c_lo:sbi * SBT + c_hi], in_=st[:, c_lo:c_hi]),
                          "stgo%d" % (oc % 2), reads=skeys)
            T.barrier()

    def phase_B():
        slopes_a, slopes_b = alibi_slopes()
        with ExitStack() as es:
            def sb(name, shape, dt=F32):
                return es.enter_context(nc.sbuf_tensor(name, shape, dt))
            QT = [[sb("QT%d_%d" % (s_, j), [128, T_OWN], BF16) for j in range(2)] for s_ in range(2)]
            KT = [[sb("KT%d_%d" % (s_, j), [128, T_EXT], BF16) for j in range(2)] for s_ in range(2)]
            VT = [sb("VT%d" % i, [128, T_EXT], BF16) for i in range(2)]
            OD = [sb("OD%d" % j, [128, T_OWN]) for j in range(2)]
            rbc = sb("rbcB", [64, T_OWN])
            sqh = [sb("sqB%d" % j, [64, T_OWN], BF16) for j in range(2)]
            ssq = [sb("ssqB%d" % i, [128, T_OWN]) for i in range(2)]
            onesf = sb("onesfB", [128, 64])
            cvals = (-64, 64, -128, 0, 128)
            AR = sb("ARc", [128, 5, 384])
            MK = sb("MKc", [128, 5, 384])
            ari = sb("ari", [128, 384], I32)
            biasT = sb("biasT", [128, 12, 384])
            tmp = [sb("tmpB%d" % i, [128, 384]) for i in range(4)]
            PT = [sb("PTB%d" % i, [128, 384], BF16) for i in range(4)]
            Vaug = [sb("Vaug%d" % i, [128, 2, 65], BF16) for i in range(2)]
            esink = sb("esink", [128, 16])

            for s_ in range(2):
                for j in range(2):
                    T.dma("pool", lambda e, s_=s_, j=j: e.dma_start(out=QT[s_][j][64:67, :], in_=segq_d[:, HALO:HALO + T_OWN]),
                          "seg%d" % (s_ * 2 + j), writes=["QTseg"])
                    T.dma("pool", lambda e, s_=s_, j=j: e.dma_start(out=KT[s_][j][64:67, :], in_=segk_d[:, :]),
                          "segk%d" % (s_ * 2 + j), writes=["KTseg"])
            conv_jobs = []
            for ti, (src, dst) in enumerate(((eu_d, eub_d), (ev_d, evb_d))):
                sv = src.rearrange("(p r) d -> p r d", p=128)
                dv = dst.rearrange("(p r) d -> p r d", p=128)
                for c in range(32):
                    conv_jobs.append((sv, dv, c))
            conv_state = [0, 0]

            def conv_tick():
                conv_state[1] += 1
                if conv_state[1] % 14 == 0 and conv_state[0] < len(conv_jobs):
                    sv, dv, c = conv_jobs[conv_state[0]]
                    T.dma("pool", lambda e: e.dma_start(out=dv[:, c * 4:(c + 1) * 4, :], in_=sv[:, c * 4:(c + 1) * 4, :]),
                          "cv%d" % (conv_state[0] % 4))
                    conv_state[0] += 1

            T.dma("sp", lambda e: e.dma_start(out=esink[:], in_=sink_d[:, :]), "sink", writes=["esink"])
            T.op("act", lambda e: e.activation(out=esink[:], in_=esink[:], func=AF.Exp), reads=["esink"], writes=["esink"])
            T.op("pool", lambda e: e.memset(onesf[:], 1.0), writes=["onesf"])
            for i in range(2):
                T.op("pool", lambda e, i=i: e.memset(Vaug[i][:], 1.0), writes=["Vaug%d" % i])
                T.op("pool", lambda e, i=i: e.memset(ssq[i][:], 0.0), writes=["ssqB%d" % i])
            for ci, c in enumerate(cvals):
                W = 64.0 if ci < 2 else 128.0
                T.op("pool", lambda e, c=c: e.iota(ari[:], [[-1, 384]], base=c, channel_multiplier=1), reads=["ari"], writes=["ari"])
                T.op("dve", lambda e, ci=ci: e.tensor_copy(out=AR[:, ci, :], in_=ari[:]), reads=["ari"], writes=["AR%d" % ci])
                T.op("act", lambda e, ci=ci: e.activation(out=AR[:, ci, :], in_=AR[:, ci, :], func=AF.Abs), reads=["AR%d" % ci], writes=["AR%d" % ci])
                T.op("dve", lambda e, ci=ci, W=W: e.tensor_scalar(out=MK[:, ci, :], in0=AR[:, ci, :], scalar1=W, scalar2=NEGM,
                                                                 op0=ALU.is_gt, op1=ALU.mult),
                     reads=["AR%d" % ci], writes=["MK%d" % ci])

            def load_hp(hp):
                slot = hp % 2
                vk = "VT%d" % slot
                if hp < 8:
                    for j in range(2):
                        T.dma("sp", lambda e, j=j: e.dma_start(out=QT[slot][j][0:64, :], in_=qkvT_d[hp * 128 + 64 * j:hp * 128 + 64 * j + 64, HALO:HALO + T_OWN]),
                              "QT%d_%d" % (slot, j), writes=["QT%d_%d" % (slot, j)])
                        T.dma("sp", lambda e, j=j: e.dma_start(out=KT[slot][j][0:64, :], in_=qkvT_d[(8 + hp) * 128 + 64 * j:(8 + hp) * 128 + 64 * j + 64, :]),
                              "KT%d_%d" % (slot, j), writes=["KT%d_%d" % (slot, j)])
                    T.dma("sp", lambda e: e.dma_start(out=VT[slot][:], in_=qkvT_d[(16 + hp) * 128:(17 + hp) * 128, :]), vk, writes=[vk])
                else:
                    hb = hp - 8
                    kv = hb // 2
                    kr = 4096 + 64 * kv
                    vr = 4352 + 64 * kv
                    for j in range(2):
                        T.dma("sp", lambda e, j=j: e.dma_start(out=QT[slot][j][0:64, :], in_=qkvT_d[(24 + hb) * 128 + 64 * j:(24 + hb) * 128 + 64 * j + 64, HALO:HALO + T_OWN]),
                              "QT%d_%d" % (slot, j), writes=["QT%d_%d" % (slot, j)])
                        T.dma("sp", lambda e, j=j: e.dma_start(out=KT[slot][j][0:64, :], in_=qkvT_d[kr:kr + 64, :]),
                              "KT%d_%d" % (slot, j), writes=["KT%d_%d" % (slot, j)])
                        T.dma("sp", lambda e, j=j: e.dma_start(out=VT[slot][64 * j:64 * j + 64, :], in_=qkvT_d[vr:vr + 64, :]), vk + "_%d" % j, writes=[vk])

            it = 0
            load_hp(0)
            for hp in range(16):
                isB = hp >= 8
                slot = hp % 2
                vk = "VT%d" % slot
                if hp + 1 < 16:
                    load_hp(hp + 1)
                if not isB:
                    pats = [(1, 64, 0, 1), (4, 64, 0, 1), (16, 64, 0, 1)]
                    slopes = [slopes_a[2 * hp], slopes_a[2 * hp + 1]]
                    heads = [2 * hp, 2 * hp + 1]
                else:
                    hb = hp - 8
                    pats = [(1, 128, 2, 3)]
                    slopes = [slopes_b[2 * hb], slopes_b[2 * hb + 1]]
                    heads = [2 * hb, 2 * hb + 1]
                bidx = {}
                nb = 0
                for j in range(2):
                    for pi, (d, W, c0, c1) in enumerate(pats):
                        cis = (c0, c1) if not isB else (2, 3, 4)
                        for ci in cis:
                            bidx[(j, pi, ci)] = nb
                            T.op("dve", lambda e, nb=nb, ci=ci, sl=-slopes[j] * d: e.scalar_tensor_tensor(
                                out=biasT[:, nb, :], in0=AR[:, ci, :], scalar=sl, in1=MK[:, ci, :], op0=ALU.mult, op1=ALU.add),
                                 reads=["AR%d" % ci, "MK%d" % ci], writes=["bias%d" % nb])
                            nb += 1
                for j in range(2):
                    T.op("pool", lambda e, j=j: e.memset(OD[j][:], 0.0), writes=["OD%d" % j])
                iters = []
                for pi, (d, W, c0, c1) in enumerate(pats):
                    ua = HALO // d
                    L = T_OWN // d
                    ub = ua + L
                    nkt = -(-(L + 2 * W) // 128)
                    for r in range(d):
                        for m in range(nkt):
                            k0 = ua - W + 128 * m
                            nk = min(128, ub + W - k0)
                            qlo = max(ua, k0 - W)
                            qhi = min(ub, k0 + nk + W)
                            nq = qhi - qlo
                            ci = cvals.index(k0 - qlo)
                            iters.append((pi, nk, nq, ci, cs(r + d * k0, nk, d), cs(r + d * qlo - HALO, nq, d)))

                def stage1(itn, desc, slot=slot, vk=vk, bidx=bidx):
                    pi, nk, nq, ci, kcols, qcols = desc
                    pb = PSB[itn % 2]
                    pbk = "psb%d" % (itn % 2)
                    va = Vaug[itn % 2]
                    vak = "Vaug%d" % (itn % 2)
                    T.op("pe", lambda e: e.transpose(pb[0:nk, 0:128], in_=VT[slot][:, kcols], identity=ident[:]),
                         reads=[vk], writes=[pbk])
                    T.op("act", lambda e: e.activation(out=va[0:nk, :, 0:64], in_=pb[0:nk, 0:128].rearrange("p (a b) -> p a b", b=64), func=AF.Copy),
                         reads=[pbk], writes=[vak])
                    for j in range(2):
                        si = (2 * itn + j) % 4
                        S = PSF[si]
                        sk_ = "psf%d" % si
                        tm = tmp[si]
                        pt = PT[si]
                        T.op("pe", lambda e, S=S, j=j: e.matmul(
                            S[0:nk, 0:nq], lhsT=KT[slot][j][0:67, kcols], rhs=QT[slot][j][0:67, qcols], start=True, stop=True),
                             reads=["KT%d_%d" % (slot, j), "QT%d_%d" % (slot, j), "KTseg", "QTseg"], writes=[sk_])
                        bi = bidx[(j, pi, ci)]
                        T.op("dve", lambda e, S=S, tm=tm, bi=bi: e.scalar_tensor_tensor(
                            out=tm[0:nk, 0:nq], in0=S[0:nk, 0:nq], scalar=0.125, in1=biasT[0:nk, bi, 0:nq], op0=ALU.mult, op1=ALU.add),
                             reads=[sk_, "bias%d" % bi], writes=["tmp%d" % si])
                        T.op("act", lambda e, tm=tm, pt=pt: e.activation(out=pt[0:nk, 0:nq], in_=tm[0:nk, 0:nq], func=AF.Exp),
                             reads=["tmp%d" % si], writes=["PT%d" % si])

                def stage2(itn, desc):
                    pi, nk, nq, ci, kcols, qcols = desc
                    va = Vaug[itn % 2]
                    vak = "Vaug%d" % (itn % 2)
                    for j in range(2):
                        si = (2 * itn + j) % 4
                        pt = PT[si]
                        Ops = PSF[4 + j]
                        opk = "psf%d" % (4 + j)
                        T.op("pe", lambda e, j=j, pt=pt, Ops=Ops: e.matmul(Ops[0:65, 0:nq], lhsT=va[0:nk, j, :], rhs=pt[0:nk, 0:nq], start=True, stop=True),
                             reads=[vak, "PT%d" % si], writes=[opk])
                        T.op("dve", lambda e, j=j, Ops=Ops: e.tensor_tensor(out=OD[j][0:65, qcols], in0=Ops[0:65, 0:nq], in1=OD[j][0:65, qcols], op=ALU.add),
                             reads=[opk, "OD%d" % j], writes=["OD%d" % j])

                stage1(it, iters[0])
                for i_ in range(len(iters)):
                    if i_ + 1 < len(iters):
                        stage1(it + i_ + 1, iters[i_ + 1])
                    stage2(it + i_, iters[i_])
                    conv_tick()
                it += len(iters)
                sidx = 1 if isB else 0
                for j in range(2):
                    odk = "OD%d" % j
                    if isB:
                        hh = heads[j]
                        T.op("dve", lambda e, j=j, hh=hh: e.tensor_scalar(out=OD[j][64:65, :], in0=OD[j][64:65, :], scalar1=esink[64:65, hh:hh + 1],
                                                                         scalar2=None, op0=ALU.add),
                             reads=[odk, "esink"], writes=[odk])
                    for g in range(T_OWN // 512):
                        pp = PSF[g % 4]
                        ppk = "psf%d" % (g % 4)
                        T.op("pe", lambda e, j=j, pp=pp, g=g: e.matmul(pp[0:64, 0:512], lhsT=onesf[64:65, 0:64], rhs=OD[j][64:65, g * 512:(g + 1) * 512],
                                                                      start=True, stop=True),
                             reads=[odk, "onesf"], writes=[ppk])
                        T.op("dve", lambda e, pp=pp, g=g: e.reciprocal(out=rbc[:, g * 512:(g + 1) * 512], in_=pp[0:64, 0:512]),
                             reads=[ppk], writes=["rbc%d" % g])
                    rkeys = ["rbc%d" % g for g in range(T_OWN // 512)]
                    T.op("dve", lambda e, j=j: e.tensor_tensor(out=OD[j][0:64, :], in0=OD[j][0:64, :], in1=rbc[:, :], op=ALU.mult),
                         reads=[odk] + rkeys, writes=[odk])
                    T.op("act", lambda e, j=j: e.activation(out=sqh[j][:], in_=OD[j][0:64, :], func=AF.Square), reads=[odk], writes=["sq%d" % j])
                    T.dma("sp", lambda e, j=j, hp=hp: e.dma_start(out=oaT_d[hp * 128 + 64 * j:hp * 128 + 64 * j + 64, :], in_=OD[j][0:64, :]), "oaT%d" % j, reads=[odk])
                for g in range(T_OWN // 512):
                    pp = PSF[4 + g % 2]
                    ppk = "psf%d" % (4 + g % 2)
                    for j in range(2):
                        T.op("pe", lambda e, pp=pp, g=g, j=j: e.matmul(pp[:, 0:512], lhsT=ones128[0:64, :], rhs=sqh[j][:, g * 512:(g + 1) * 512],
                                                                      start=(j == 0), stop=(j == 1)),
                             reads=["sq%d" % j], writes=[ppk])
                    T.op("dve", lambda e, pp=pp, g=g, sidx=sidx: e.tensor_tensor(out=ssq[sidx][:, g * 512:(g + 1) * 512], in0=pp[:, 0:512],
                                                                                 in1=ssq[sidx][:, g * 512:(g + 1) * 512], op=ALU.add),
                         reads=[ppk, "ssqB%d" % sidx], writes=["ssqB%d" % sidx])
            while conv_state[0] < len(conv_jobs):
                conv_state[1] = 13
                conv_tick()
            for i in range(2):
                T.op("act", lambda e, i=i: e.activation(out=ssq[i][:], in_=ssq[i][:], func=AF.Sqrt, scale=1.0 / 1024.0, bias=epsc[:, 0:1]),
                     reads=["ssqB%d" % i], writes=["ssqB%d" % i])
                T.op("dve", lambda e, i=i: e.reciprocal(out=ssq[i][:], in_=ssq[i][:]), reads=["ssqB%d" % i], writes=["ssqB%d" % i])
                T.dma("sp", lambda e, i=i: e.dma_start(out=rst_d[i], in_=ssq[i][:]), "rst%d" % i, reads=["ssqB%d" % i])
            T.barrier()

    def phase_C():
        with ExitStack() as es:
            def sb(name, shape, dt=F32):
                return es.enter_context(nc.sbuf_tensor(name, shape, dt))
            wout = sb("wout", [128, KC, D], BF16)
            rst = sb("rstC", [128, 2, T_OWN])
            gab = sb("gabC", [128, 16])
            ld = sb("ldC", [128, KC, 512])
            mixedT = [sb("mixT%d" % i, [128, KC, 512], BF16) for i in range(2)]
            xin = [sb("xinC%d" % i, [128, D]) for i in range(2)]
            x1t = [sb("x1tC%d" % i, [128, D]) for i in range(2)]
            for kc in range(KC):
                for h in range(2):
                    T.dma("pool", lambda e, kc=kc, h=h: e.dma_start(out=wout[:, kc, h * 1024:(h + 1) * 1024],
                                                                      in_=w_out_d[kc * 128:(kc + 1) * 128, h * 1024:(h + 1) * 1024]),
                          "wo%d" % ((2 * kc + h) % 4), writes=["wout%d_%d" % (kc, h)])
            woutkeys = ["wout%d_%d" % (kc, h) for kc in range(KC) for h in range(2)]
            for i in range(2):
                T.dma("sp", lambda e, i=i: e.dma_start(out=rst[:, i, :], in_=rst_d[i]), "rstl%d" % i, writes=["rst"])
            T.dma("sp", lambda e: e.dma_start(out=gab[:], in_=gab_d[:, :]), "gab", writes=["gab"])
            oaT_v = oaT_d.rearrange("(c p) t -> p c t", p=128)
            T.dma("sp", lambda e: e.dma_start(out=ld[:], in_=oaT_v[:, :, 0:512]), "ldC", writes=["ld"])
            for tg in range(T_OWN // 512):
                mx = mixedT[tg % 2]
                mk_ = "mix%d" % (tg % 2)
                for c in range(16):
                    T.op("dve", lambda e, mx=mx, c=c, tg=tg: e.scalar_tensor_tensor(out=mx[:, c, :], in0=ld[:, c, :], scalar=gab[:, c:c + 1],
                                                                                  in1=rst[:, c // 8, tg * 512:(tg + 1) * 512], op0=ALU.mult, op1=ALU.mult),
                         reads=["ld", "gab", "rst"], writes=[mk_ + "_%d" % c])
                mkeys = [mk_ + "_%d" % c for c in range(16)]
                if tg + 1 < T_OWN // 512:
                    T.dma("sp", lambda e, tg=tg: e.dma_start(out=ld[:], in_=oaT_v[:, :, (tg + 1) * 512:(tg + 2) * 512]), "ldC", writes=["ld"])
                for tl in range(4):
                    t = tg * 4 + tl
                    xi = xin[t % 2]
                    xk = "xinC%d" % (t % 2)
                    xo = x1t[t % 2]
                    xok = "x1t%d" % (t % 2)
                    T.dma("sp", lambda e, xi=xi, t=t: e.dma_start(out=xi[:], in_=x_ext[HALO + t * 128:HALO + (t + 1) * 128, :]), xk, writes=[xk])
                    okeys = []
                    for ng in range(4):
                        bi = (t * 4 + ng) % 6
                        acc = PSF[bi]
                        for kc in range(KC):
                            T.op("pe", lambda e, acc=acc, mx=mx, kc=kc, tl=tl, ng=ng: e.matmul(acc[:, 0:512], lhsT=mx[:, kc, tl * 128:(tl + 1) * 128],
                                                                                           rhs=wout[:, kc, ng * 512:(ng + 1) * 512],
                                                                                           start=(kc == 0), stop=(kc == KC - 1)),
                                 reads=mkeys + woutkeys, writes=["psf%d" % bi])
                        T.op("dve", lambda e, acc=acc, xo=xo, xi=xi, ng=ng: e.tensor_tensor(out=xo[:, ng * 512:(ng + 1) * 512], in0=acc[:, 0:512],
                                                                                          in1=xi[:, ng * 512:(ng + 1) * 512], op=ALU.add),
                             reads=["psf%d" % bi, xk], writes=[xok + "_%d" % ng])
                        okeys.append(xok + "_%d" % ng)
                    T.dma("sp", lambda e, xo=xo, t=t: e.dma_start(out=x1_d[t * 128:(t + 1) * 128, :], in_=xo[:]), "x1o%d" % (t % 2), reads=okeys)
            T.barrier()

    def phase_DE(do_E=True):
        with ExitStack() as es_o:
            def osb(name, shape, dt=F32):
                return es_o.enter_context(nc.sbuf_tensor(name, shape, dt))
            eid2 = osb("eid2", [128, 3, 128], I32)
            gat2 = osb("gat2", [128, 3, 128], F32)
            x1j = osb("x1jD", [128, D], F32)
            g2 = osb("g2D", [128, D], F32)
            x1i = osb("x1iD", [128, D], F32)
            ssq = osb("ssqD", [128, 3 * NT_OWN], F32)
            rs = osb("rsD", [128, 3 * NT_OWN], F32)
            junk = osb("junkD", [128, D], BF16)
            T.dma("sp", lambda e: e.dma_start(out=g2[:], in_=g2_d[:, :]), "g2", writes=["g2"])
            T.op("pool", lambda e: e.memset(ssq[:], 0.0), writes=["ssqDz"])
            with ExitStack() as es:
                def sb(name, shape, dt=F32):
                    return es.enter_context(nc.sbuf_tensor(name, shape, dt))
                wpq = sb("wpq", [128, KC, D], BF16)
                skb = sb("skb", [128, 16, 128], BF16)
                skT = sb("skT", [128, 16, 128], BF16)
                hnb = sb("hnbD", [128, D], BF16)
                hnT = sb("hnTD", [128, KC, 512], BF16)
                qT = sb("qTD", [128, 16, 512], BF16)
                scs = [sb("scsD%d" % i, [128, D]) for i in range(2)]
                for kc in range(KC):
                    for h in range(2):
                        T.dma("pool", lambda e, kc=kc, h=h: e.dma_start(out=wpq[:, kc, h * 1024:(h + 1) * 1024],
                                                                          in_=w_pq_d[kc * 128:(kc + 1) * 128, h * 1024:(h + 1) * 1024]),
                              "wq%d" % ((2 * kc + h) % 4), writes=["wpq%d_%d" % (kc, h)])
                wpqkeys = ["wpq%d_%d" % (kc, h) for kc in range(KC) for h in range(2)]
                T.dma("pool", lambda e: e.dma_start(out=skb[:], in_=sk_d.rearrange("(c n) d -> n c d", n=128)), "skb", writes=["skb"])
                for half in range(2):
                    pb = PSB[half]
                    for k8 in range(8):
                        c = half * 8 + k8
                        T.op("pe", lambda e, pb=pb, k8=k8, c=c: e.transpose(pb[:, k8 * 128:(k8 + 1) * 128], in_=skb[:, c, :], identity=ident[:]),
                             reads=["skb"], writes=["psb%d" % half])
                    T.op("dve", lambda e, pb=pb, half=half: e.tensor_copy(out=skT[:, half * 8:half * 8 + 8, :], in_=pb[:].rearrange("p (k t) -> p k t", k=8)),
                         reads=["psb%d" % half], writes=["skT%d" % half])
                for tgp in range(NT_OWN // 4):
                    for tl in range(4):
                        t = tgp * 4 + tl
                        T.dma("sp", lambda e, t=t: e.dma_start(out=x1i[:], in_=x1_d[t * 128:(t + 1) * 128, :]), "x1i", writes=["x1i"])
                        T.op("act", lambda e, t=t: e.activation(out=junk[:], in_=x1i[:], func=AF.Square, accum_out=ssq[:, t:t + 1]),
                             reads=["x1i", "ssqDz"], writes=["ssqD%d" % t])
                        T.op("act", lambda e, t=t: e.activation(out=rs[:, t:t + 1], in_=ssq[:, t:t + 1], func=AF.Sqrt, scale=1.0 / D, bias=epsc[:, 0:1]),
                             reads=["ssqD%d" % t], writes=["rsD%d" % t])
                        T.op("dve", lambda e, t=t: e.reciprocal(out=rs[:, t:t + 1], in_=rs[:, t:t + 1]), reads=["rsD%d" % t], writes=["rsD%d" % t])
                        T.op("dve", lambda e, t=t: e.scalar_tensor_tensor(out=hnb[:], in0=x1i[:], scalar=rs[:, t:t + 1], in1=g2[:], op0=ALU.mult, op1=ALU.mult),
                             reads=["x1i", "rsD%d" % t, "g2"], writes=["hnb"])
                        transpose16(hnb, "hnb", lambda half, tl=tl: hnT[:, half * 8:half * 8 + 8, tl * 128:(tl + 1) * 128], "hnT%d" % tl)
                    hkeys = ["hnT%d" % tl for tl in range(4)]
                    for c in range(16):
                        bi = c % 4
                        acc = PSF[bi]
                        for kc in range(KC):
                            T.op("pe", lambda e, acc=acc, c=c, kc=kc: e.matmul(acc[:, 0:512], lhsT=wpq[:, kc, c * 128:(c + 1) * 128],
                                                                             rhs=hnT[:, kc, :], start=(kc == 0), stop=(kc == KC - 1)),
                                 reads=wpqkeys + hkeys, writes=["psf%d" % bi])
                        if c % 2 == 0:
                            T.op("act", lambda e, acc=acc, c=c: e.activation(out=qT[:, c, :], in_=acc[:, 0:512], func=AF.Copy),
                                 reads=["psf%d" % bi], writes=["qT%d" % c])
                        else:
                            T.op("dve", lambda e, acc=acc, c=c: e.tensor_copy(out=qT[:, c, :], in_=acc[:, 0:512]),
                                 reads=["psf%d" % bi], writes=["qT%d" % c])
                    for tl in range(4):
                        t = tgp * 4 + tl
                        sc_ = scs[t % 2]
                        sck = "scs%d" % (t % 2)
                        okeys = []
                        for cg in range(4):
                            bi = 4 + cg % 2
                            acc = PSF[bi]
                            for ci in range(4):
                                c = cg * 4 + ci
                                T.op("pe", lambda e, acc=acc, ci=ci, c=c, tl=tl: e.matmul(acc[:, ci * 128:(ci + 1) * 128], lhsT=qT[:, c, tl * 128:(tl + 1) * 128],
                                                                                         rhs=skT[:, c, :], start=True, stop=True),
                                     reads=["qT%d" % c, "skT0", "skT1"], writes=["psf%d" % bi])
                            if cg % 2 == 0:
                                T.op("act", lambda e, acc=acc, cg=cg, sc_=sc_: e.activation(out=sc_[:, cg * 512:(cg + 1) * 512], in_=acc[:, 0:512], func=AF.Copy),
                                     reads=["psf%d" % bi], writes=[sck + "_%d" % cg])
                            else:
                                T.op("dve", lambda e, acc=acc, cg=cg, sc_=sc_: e.tensor_copy(out=sc_[:, cg * 512:(cg + 1) * 512], in_=acc[:, 0:512]),
                                     reads=["psf%d" % bi], writes=[sck + "_%d" % cg])
                            okeys.append(sck + "_%d" % cg)
                        T.dma("sp", lambda e, t=t, sc_=sc_: e.dma_start(out=scr_d[t * 128:(t + 1) * 128, :], in_=sc_[:]), "scso%d" % (t % 2), reads=okeys)
                T.barrier()
            if not do_E:
                return
            with ExitStack() as es:
                def sb(name, shape, dt=F32):
                    return es.enter_context(nc.sbuf_tensor(name, shape, dt))
                W1 = sb("W1", [128, D])
                W2 = sb("W2", [128, D])
                W3 = sb("W3", [128, D])
                W4 = sb("W4", [128, D])
                v16 = sb("v16", [128, 16, 16])
                i16 = sb("i16", [128, 16, 16], U32)
                i16f = sb("i16f", [128, 16, 16])
                top = sb("topD", [128, 8, 16])
                pos = sb("posD", [128, 8, 16], U32)
                posf = sb("posfD", [128, 8, 16])
                af_ = sb("afD", [128, 8, 16])
                bf_ = sb("bfD", [128, 8, 16])
                s1 = sb("s1D", [128, 8, 16])
                s2 = sb("s2D", [128, 8, 16])
                eg = sb("egD", [128, 8, 16])
                esum = sb("esumD", [128, 8])
                io16i = sb("io16i", [128, 16], I32)
                io16 = sb("io16", [128, 16])
                thr15 = sb("thr15", [128, 15])
                NS = 16
                rows = [sb("rowE%d" % i, [128, D], BF16) for i in range(NS)]
                ND = 4
                diag = [sb("diagE%d" % i, [128, 128], BF16) for i in range(ND)]
                gf = sb("gfE", [128, D])
                hnb = sb("hnbE", [128, D], BF16)
                yacc = sb("yaccE", [128, D])
                dots = sb("dotsE", [128, 128])
                wgt = sb("wgtE", [128, 128])
                junkf = sb("junkfE", [128, D], BF16)
                junka = sb("junkaE", [128, D], BF16)
                NP = 4
                prod = [sb("prodE%d" % i, [128, D], BF16) for i in range(NP)]
                T.dma("sp", lambda e: e.dma_start(out=gf[:], in_=gf_d[:, :]), "gf", writes=["gf"])
                T.op("pool", lambda e: e.iota(io16i[:], [[1, 16]], base=0, channel_multiplier=0), writes=["io16i"])
                T.op("dve", lambda e: e.tensor_copy(out=io16[:], in_=io16i[:]), reads=["io16i"], writes=["io16"])
                T.op("dve", lambda e: e.tensor_scalar(out=thr15[:], in0=io16[:, 0:15], scalar1=16.0, scalar2=16.0, op0=ALU.mult, op1=ALU.add),
                     reads=["io16"], writes=["thr15"])

                def topk(t):
                    sl = t % 3
                    T.dma("sp", lambda e: e.dma_start(out=W1[:], in_=scr_d[t * 128:(t + 1) * 128, :]), "scl", writes=["W1"])
                    sc = W1[:].rearrange("p (c n) -> p c n", c=16)
                    sc2 = W2[:].rearrange("p (c n) -> p c n", c=16)
                    for c in range(16):
                        T.op("dve", lambda e, c=c: e.max(out=v16[:, c, 0:8], in_=sc[:, c, :]), reads=["W1"], writes=["v16"])
                        T.op("dve", lambda e, c=c: e.max_index(out=i16[:, c, 0:8], in_max=v16[:, c, 0:8], in_values=sc[:, c, :]), reads=["W1", "v16"], writes=["i16"])
                        T.op("dve", lambda e, c=c: e.match_replace(out=sc2[:, c, :], in_to_replace=v16[:, c, 0:8], in_values=sc[:, c, :], imm_value=-1e30),
                             reads=["W1", "v16"], writes=["W2"])
                        T.op("dve", lambda e, c=c: e.max(out=v16[:, c, 8:16], in_=sc2[:, c, :]), reads=["W2", "v16"], writes=["v16"])
                        T.op("dve", lambda e, c=c: e.max_index(out=i16[:, c, 8:16], in_max=v16[:, c, 8:16], in_values=sc2[:, c, :]), reads=["W2", "v16", "i16"], writes=["i16"])
                    v4 = v16[:].rearrange("p (h s) k -> p h s k", s=2)
                    cand = W3[:].rearrange("p (h a b) -> p h a b", h=8, a=16)
                    T.op("dve", lambda e: e.tensor_tensor(out=cand, in0=v4[:, :, 0, :].unsqueeze(3).to_broadcast([128, 8, 16, 16]),
                                                          in1=v4[:, :, 1, :].unsqueeze(2).to_broadcast([128, 8, 16, 16]), op=ALU.add),
                         reads=["v16"], writes=["W3"])
                    c3 = W3[:].rearrange("p (h n) -> p h n", h=8)
                    c4 = W4[:].rearrange("p (h n) -> p h n", h=8)
                    for h in range(8):
                        T.op("dve", lambda e, h=h: e.max(out=top[:, h, 0:8], in_=c3[:, h, :]), reads=["W3"], writes=["top"])
                        T.op("dve", lambda e, h=h: e.max_index(out=pos[:, h, 0:8], in_max=top[:, h, 0:8], in_values=c3[:, h, :]), reads=["W3", "top"], writes=["pos"])
                        T.op("dve", lambda e, h=h: e.match_replace(out=c4[:, h, :], in_to_replace=top[:, h, 0:8], in_values=c3[:, h, :], imm_value=-1e30),
                             reads=["W3", "top"], writes=["W4"])
                        T.op("dve", lambda e, h=h: e.max(out=top[:, h, 8:16], in_=c4[:, h, :]), reads=["W4", "top"], writes=["top"])
                        T.op("dve", lambda e, h=h: e.max_index(out=pos[:, h, 8:16], in_max=top[:, h, 8:16], in_values=c4[:, h, :]), reads=["W4", "top", "pos"], writes=["pos"])
                    T.op("dve", lambda e: e.tensor_copy(out=posf[:], in_=pos[:]), reads=["pos"], writes=["posf"])
                    T.op("dve", lambda e: e.tensor_copy(out=i16f[:], in_=i16[:]), reads=["i16"], writes=["i16f"])
                    big = W1[:].rearrange("p (h k a) -> p h k a", h=8, k=16)
                    big2 = W2[:].rearrange("p (h k a) -> p h k a", h=8, k=16)
                    T.op("dve", lambda e: e.tensor_tensor(out=big[:, :, :, 0:15], in0=posf[:].unsqueeze(3).to_broadcast([128, 8, 16, 15]),
                                                          in1=thr15[:].unsqueeze(1).unsqueeze(1).to_broadcast([128, 8, 16, 15]), op=ALU.is_ge),
                         reads=["posf", "thr15", "W1"], writes=["W1"])
                    T.op("dve", lambda e: e.tensor_reduce(out=af_[:], in_=big[:, :, :, 0:15], axis=AX.X, op=ALU.add), reads=["W1"], writes=["af"])
                    T.op("dve", lambda e: e.scalar_tensor_tensor(out=bf_[:], in0=af_[:], scalar=-16.0, in1=posf[:], op0=ALU.mult, op1=ALU.add),
                         reads=["af", "posf"], writes=["bf"])
                    i4 = i16f[:].rearrange("p (h s) k -> p h s k", s=2)
                    for (src, side, dst, dk) in ((af_, 0, s1, "s1"), (bf_, 1, s2, "s2")):
                        sk_ = "af" if side == 0 else "bf"
                        T.op("dve", lambda e, src=src: e.tensor_tensor(out=big, in0=src[:].unsqueeze(3).to_broadcast([128, 8, 16, 16]),
                                                                       in1=io16[:].unsqueeze(1).unsqueeze(1).to_broadcast([128, 8, 16, 16]), op=ALU.is_equal),
                             reads=[sk_, "io16", "W1"], writes=["W1"])
                        T.op("dve", lambda e, side=side: e.tensor_tensor(out=big2, in0=big, in1=i4[:, :, side, :].unsqueeze(2).to_broadcast([128, 8, 16, 16]), op=ALU.mult),
                             reads=["W1", "i16f", "W2"], writes=["W2"])
                        T.op("dve", lambda e, dst=dst: e.tensor_reduce(out=dst[:], in_=big2, axis=AX.X, op=ALU.add), reads=["W2"], writes=[dk])
                    T.op("dve", lambda e: e.scalar_tensor_tensor(out=s1[:], in0=s1[:], scalar=128.0, in1=s2[:], op0=ALU.mult, op1=ALU.add),
                         reads=["s1", "s2"], writes=["s1"])
                    T.op("dve", lambda e: e.tensor_copy(out=eid2[:, sl, :], in_=s1[:].rearrange("p h k -> p (h k)")), reads=["s1"], writes=["eid%d" % sl])
                    T.op("dve", lambda e: e.tensor_tensor(out=eg[:], in0=top[:], in1=top[:, :, 0:1].to_broadcast([128, 8, 16]), op=ALU.subtract),
                         reads=["top"], writes=["eg"])
                    T.op("act", lambda e: e.activation(out=eg[:], in_=eg[:], func=AF.Exp), reads=["eg"], writes=["eg"])
                    T.op("dve", lambda e: e.tensor_reduce(out=esum[:], in_=eg[:], axis=AX.X, op=ALU.add), reads=["eg"], writes=["esum"])
                    T.op("dve", lambda e: e.reciprocal(out=esum[:], in_=esum[:]), reads=["esum"], writes=["esum"])
                    T.op("dve", lambda e: e.tensor_tensor(out=gat2[:, sl, :].rearrange("p (h k) -> p h k", h=8), in0=eg[:],
                                                          in1=esum[:].unsqueeze(2).to_broadcast([128, 8, 16]), op=ALU.mult),
                         reads=["eg", "esum"], writes=["gat%d" % sl])
                    if debug:
                        T.dma("sp", lambda e: e.dma_start(out=dbg_d[t, :, 0:128], in_=eid2[:, sl, :].bitcast(F32)), "dbg0", reads=["eid%d" % sl])
                        T.dma("sp", lambda e: e.dma_start(out=dbg_d[t, :, 128:256], in_=gat2[:, sl, :]), "dbg1", reads=["gat%d" % sl])

                dkeys = ["dots%d" % i for i in range(128)]
                ykeys = ["yacc%d" % ng for ng in range(4)]
                x1s = [x1i, x1j]
                cnt = {"gi": 0, "di": 0, "pi": 0}

                def topk_ops(t):
                    saved = (T.op, T.dma)
                    lst = []
                    T.op = lambda *a, **k: lst.append(("op", a, k))
                    T.dma = lambda *a, **k: lst.append(("dma", a, k))
                    try:
                        topk(t)
                    finally:
                        T.op, T.dma = saved
                    return lst

                def play(lst, n):
                    for _ in range(n):
                        if not lst:
                            return
                        kind, a, k = lst.pop(0)
                        (T.op if kind == "op" else T.dma)(*a, **k)

                def load_x1(t):
                    xs = x1s[t % 2]
                    T.dma("sp", lambda e: e.dma_start(out=xs[:], in_=x1_d[t * 128:(t + 1) * 128, :]), "x1s%d" % (t % 2), writes=["x1s%d" % (t % 2)])

                def make_hnb(t):
                    xs = x1s[t % 2]
                    T.op("dve", lambda e: e.scalar_tensor_tensor(out=hnb[:], in0=xs[:], scalar=rs[:, t:t + 1], in1=g2[:], op0=ALU.mult, op1=ALU.mult),
                         reads=["x1s%d" % (t % 2), "g2"], writes=["hnb"])

                def next_row():
                    s_ = cnt["gi"] % NS
                    cnt["gi"] += 1
                    return rows[s_], "row%d" % s_

                def u_row(t, hk):
                    sl = t % 3
                    rw, rk = next_row()
                    T.dma("pool", lambda e: e.indirect_dma_start(
                        out=rw[:], out_offset=None, in_=eub_d[:, :], in_offset=bass.IndirectOffsetOnAxis(ap=eid2[:, sl, hk:hk + 1], axis=0)),
                          rk, reads=["eid%d" % sl], writes=[rk])
                    pr = prod[cnt["pi"] % NP]
                    pk = "prod%d" % (cnt["pi"] % NP)
                    cnt["pi"] += 1
                    T.op("dve", lambda e: e.tensor_tensor(out=pr[:], in0=rw[:], in1=hnb[:], op=ALU.mult), reads=[rk, "hnb"], writes=[pk])
                    if hk % 5 != 4:
                        T.op("act", lambda e: e.activation(out=junka[:], in_=pr[:], func=AF.Copy, accum_out=dots[:, hk:hk + 1]),
                             reads=[pk, "dotsz"], writes=["dots%d" % hk])
                    else:
                        T.op("dve", lambda e: e.tensor_scalar(out=junkf[:], in0=pr[:], scalar1=1.0, scalar2=0.0, op0=ALU.mult, op1=ALU.add,
                                                              accum_out=dots[:, hk:hk + 1]),
                             reads=[pk, "dotsz"], writes=["dots%d" % hk])

                def v_row(t, hk):
                    sl = t % 3
                    rw, rk = next_row()
                    dg = diag[cnt["di"] % ND]
                    dgk = "diag%d" % (cnt["di"] % ND)
                    cnt["di"] += 1
                    T.dma("pool", lambda e: e.indirect_dma_start(
                        out=rw[:], out_offset=None, in_=evb_d[:, :], in_offset=bass.IndirectOffsetOnAxis(ap=eid2[:, sl, hk:hk + 1], axis=0)),
                          rk, reads=["eid%d" % sl], writes=[rk])
                    T.op("act", lambda e: e.activation(out=dg[:], in_=ident[:], func=AF.Copy, scale=wgt[:, hk:hk + 1]), reads=["wgt"], writes=[dgk])
                    for ng in range(4):
                        T.op("pe", lambda e, ng=ng: e.matmul(PSF[ng][:, 0:512], lhsT=dg[:], rhs=rw[:, ng * 512:(ng + 1) * 512],
                                                            start=(hk == 0), stop=(hk == 127)),
                             reads=[dgk, rk], writes=["psf%d" % ng])

                def make_wgt(t):
                    sl = t % 3
                    T.op("act", lambda e: e.activation(out=wgt[:], in_=dots[:], func=AF.Gelu), reads=dkeys, writes=["wgt"])
                    T.op("dve", lambda e: e.tensor_tensor(out=wgt[:], in0=wgt[:], in1=gat2[:, sl, :], op=ALU.mult), reads=["wgt", "gat%d" % sl], writes=["wgt"])

                def finish(t):
                    xs = x1s[t % 2]
                    xk = "x1s%d" % (t % 2)
                    for ng in range(4):
                        T.op("dve", lambda e, ng=ng: e.tensor_tensor(out=yacc[:, ng * 512:(ng + 1) * 512], in0=PSF[ng][:, 0:512], in1=xs[:, ng * 512:(ng + 1) * 512], op=ALU.add),
                             reads=["psf%d" % ng, xk], writes=["yacc%d" % ng])
                    c2 = NT_OWN + t
                    T.op("act", lambda e: e.activation(out=junk[:], in_=yacc[:], func=AF.Square, accum_out=ssq[:, c2:c2 + 1]),
                         reads=ykeys + ["ssqDz"], writes=["ssqE%d" % t])
                    T.op("act", lambda e: e.activation(out=rs[:, c2:c2 + 1], in_=ssq[:, c2:c2 + 1], func=AF.Sqrt, scale=1.0 / D, bias=epsc[:, 0:1]),
                         reads=["ssqE%d" % t], writes=["rsE%d" % t])
                    T.op("dve", lambda e: e.reciprocal(out=rs[:, c2:c2 + 1], in_=rs[:, c2:c2 + 1]), reads=["rsE%d" % t], writes=["rsE%d" % t])
                    T.op("dve", lambda e: e.scalar_tensor_tensor(out=yacc[:], in0=yacc[:], scalar=rs[:, c2:c2 + 1], in1=gf[:], op0=ALU.mult, op1=ALU.mult),
                         reads=ykeys + ["rsE%d" % t, "gf"], writes=ykeys)
                    T.dma("sp", lambda e: e.dma_start(out=y_d[t * 128:(t + 1) * 128, :], in_=yacc[:]), "yout", reads=ykeys)

                def zero_dots():
                    T.op("pool", lambda e: e.memset(dots[:], 0.0), writes=["dotsz"] + dkeys)

                load_x1(0)
                load_x1(1)
                play(topk_ops(0), 10 ** 6)
                play(topk_ops(1), 10 ** 6)
                make_hnb(0)
                zero_dots()
                for hk in range(128):
                    u_row(0, hk)
                make_wgt(0)
                make_hnb(1)
                for t in range(NT_OWN):
                    pend = topk_ops(t + 2) if t + 2 < NT_OWN else []
                    if t + 1 < NT_OWN:
                        zero_dots()
                    for hk in range(128):
                        v_row(t, hk)
                        if t + 1 < NT_OWN:
                            u_row(t + 1, hk)
                        play(pend, 2)
                    play(pend, 10 ** 6)
                    finish(t)
                    if t + 1 < NT_OWN:
                        make_wgt(t + 1)
                    if t + 2 < NT_OWN:
                        load_x1(t + 2)
                        make_hnb(t + 2)
                T.barrier()

    if "A" in phases:
        phase_A()
    if "B" in phases:
        phase_B()
    if "C" in phases:
        phase_C()
    if "D" in phases:
        phase_DE(do_E=("E" in phases))
    T.barrier()
    gs.close()
    return nc, T


def make_in_maps(x_prompt, x_sample, norm1_g, w_in, a_out_g, b_out_g, sink, w_out, norm2_g, w_pq,
                 sub_keys, expert_u, expert_v, normf_g):
    f = np.float32
    xs = np.concatenate([np.asarray(x_prompt, f).reshape(-1, D), np.asarray(x_sample, f).reshape(-1, D)], axis=0)
    ntot = xs.shape[0]
    assert ntot == NCORES * T_OWN
    xpad = np.zeros((ntot + 2 * HALO, D), f)
    xpad[HALO:HALO + ntot] = xs
    nprompt = np.asarray(x_prompt).shape[0] * np.asarray(x_prompt).shape[1]
    seqlen_p = np.asarray(x_prompt).shape[1]
    seg = np.full((ntot + 2 * HALO,), 6.0, f)
    ids = np.arange(ntot)
    seg[HALO:HALO + ntot] = np.where(ids < nprompt, ids // seqlen_p, nprompt // seqlen_p).astype(f)
    segq = np.stack([seg * seg, seg, np.ones_like(seg)]).astype(f)
    segk = np.stack([-SEG_B * np.ones_like(seg), 2 * SEG_B * seg, -SEG_B * seg * seg]).astype(f)
    rep = lambda v: np.ascontiguousarray(np.broadcast_to(np.asarray(v, f).reshape(1, D), (128, D)))
    gcat = np.concatenate([np.asarray(a_out_g, f).reshape(-1), np.asarray(b_out_g, f).reshape(-1)])
    gab = np.ascontiguousarray(gcat.reshape(16, 128).T)
    sk = np.asarray(sink, f).reshape(-1)
    sinkc = np.ascontiguousarray(np.broadcast_to(sk.reshape(1, 16), (128, 16)))
    shared = {
        "g1b": rep(norm1_g), "g2b": rep(norm2_g), "gfb": rep(normf_g), "gab": gab, "sinkc": sinkc,
        "w_in": np.ascontiguousarray(np.asarray(w_in, f).reshape(D, IN_W)),
        "w_out": np.ascontiguousarray(np.asarray(w_out, f).reshape(D, D)),
        "w_pq": np.ascontiguousarray(np.asarray(w_pq, f).reshape(D, D)),
        "sub_keys": np.ascontiguousarray(np.asarray(sub_keys, f).reshape(16 * 128, 128)),
        "expert_u": np.ascontiguousarray(np.asarray(expert_u, f).reshape(16384, D)),
        "expert_v": np.ascontiguousarray(np.asarray(expert_v, f).reshape(16384, D)),
    }
    in_maps = []
    for c in range(NCORES):
        lo = c * T_OWN
        m = dict(shared)
        m["x_ext"] = np.ascontiguousarray(xpad[lo:lo + T_EXT])
        m["segq"] = np.ascontiguousarray(segq[:, lo:lo + T_EXT])
        m["segk"] = np.ascontiguousarray(segk[:, lo:lo + T_EXT])
        in_maps.append(m)
    return in_maps


def kernel(x_prompt, x_sample, norm1_g, w_in, a_out_g, b_out_g, sink, w_out, norm2_g, w_pq,
           sub_keys, expert_u, expert_v, normf_g):
    in_maps = make_in_maps(x_prompt, x_sample, norm1_g, w_in, a_out_g, b_out_g, sink, w_out, norm2_g, w_pq,
                           sub_keys, expert_u, expert_v, normf_g)
    nc, _ = build_program()
    res = run_bass_kernel_spmd(nc, in_maps, core_ids=list(range(NCORES)))
    y = np.concatenate([np.asarray(r["y"], np.float32) for r in res.results], axis=0)
    xp = np.asarray(x_prompt)
    xsm = np.asarray(x_sample)
    npr = xp.shape[0] * xp.shape[1]
    return (np.ascontiguousarray(y[:npr]).reshape(xp.shape), np.ascontiguousarray(y[npr:]).reshape(xsm.shape))
```

```python
import math
from contextlib import ExitStack

import numpy as np
import concourse.bass as bass
import concourse.mybir as mybir
from concourse.bass_utils import run_bass_kernel_spmd

F32 = mybir.dt.float32
BF16 = mybir.dt.bfloat16
I32 = mybir.dt.int32
U32 = mybir.dt.uint32
AF = mybir.ActivationFunctionType
ALU = mybir.AluOpType
AX = mybir.AxisListType

NCORES = 8
D = 2048
KC = 16
T_OWN = 3072
HALO = 1024
T_EXT = T_OWN + 2 * HALO
NT_OWN = T_OWN // 128
NT_EXT = T_EXT // 128
SBT = T_EXT // 2
IN_W = 4608
NOC = IN_W // 128
EPS = 1e-6
SEG_B = float(2 ** 17)
NEGM = -30000.0


class _Op:
    __slots__ = ("eng", "fn", "is_dma", "semkey", "deps", "needed", "token", "emitted", "idx")

    def __init__(self, eng, fn, is_dma, semkey):
        self.eng = eng
        self.fn = fn
        self.is_dma = is_dma
        self.semkey = semkey
        self.deps = []
        self.needed = False
        self.token = None
        self.emitted = False
        self.idx = 0


class Tracker:
    ENGS = ("pe", "act", "dve", "pool", "sp")

    def __init__(self, nc):
        self.nc = nc
        self.eng = {"pe": nc.tensor, "act": nc.scalar, "dve": nc.vector,
                    "pool": nc.gpsimd, "sp": nc.sync}
        self.esem = {e: nc.semaphore("s_" + e).__enter__() for e in self.ENGS}
        self.ecount = {e: 0 for e in self.ENGS}
        self.dsem = {}
        self.dlast = {}
        self.pending = []
        self.last_w = {}
        self.readers = {}
        self.known = {e: {} for e in self.ENGS}
        self.last_op = {e: None for e in self.ENGS}
        self.n_ops = 0
        self.n_waits = 0
        self.eidx = {e: 0 for e in self.ENGS}
        self.embed = True

    def _record(self, op, reads, writes):
        if not op.is_dma:
            self.eidx[op.eng] += 1
            op.idx = self.eidx[op.eng]
        deps = []
        for k in reads:
            w = self.last_w.get(k)
            if w is not None:
                deps.append(w)
        for k in writes:
            w = self.last_w.get(k)
            if w is not None:
                deps.append(w)
            deps.extend(self.readers.get(k, ()))
        if op.is_dma:
            prev = self.dlast.get(op.semkey)
            if prev is not None:
                deps.append(prev)
            self.dlast[op.semkey] = op
        seen = set()
        for d in deps:
            if d is op or id(d) in seen:
                continue
            seen.add(id(d))
            if (not d.is_dma) and (not op.is_dma) and d.eng == "pe" and op.eng == "pe":
                continue
            if (not d.is_dma) and (not op.is_dma) and d.eng == op.eng and op.eng in ("dve", "act") \
                    and op.idx - d.idx >= 3:
                continue
            d.needed = True
            op.deps.append(d)
        for k in writes:
            self.last_w[k] = op
            self.readers[k] = []
        for k in reads:
            if k in writes:
                continue
            lst = self.readers.setdefault(k, [])
            if not op.is_dma:
                lst[:] = [r for r in lst if r.is_dma or r.eng != op.eng]
            lst.append(op)
        self.pending.append(op)
        if not op.is_dma:
            self.last_op[op.eng] = op
        self.n_ops += 1
        return op

    def op(self, eng, fn, reads=(), writes=()):
        return self._record(_Op(eng, fn, False, None), tuple(reads), tuple(writes))

    def dma(self, eng, fn, semkey, reads=(), writes=()):
        return self._record(_Op(eng, fn, True, semkey), tuple(reads), tuple(writes))

    def _wait(self, engname, sem, cnt):
        kn = self.known[engname]
        key = id(sem)
        if kn.get(key, 0) >= cnt:
            return
        self.eng[engname].wait_ge(sem, cnt)
        kn[key] = cnt
        self.n_waits += 1

    def flush(self):
        for e in self.ENGS:
            if self.last_op[e] is not None and not self.last_op[e].emitted:
                self.last_op[e].needed = True
        for op in self.pending:
            need = {}
            for d in op.deps:
                assert d.emitted and d.token is not None, "dep without token"
                sem, cnt = d.token
                if need.get(id(sem), (None, 0))[1] < cnt:
                    need[id(sem)] = (sem, cnt)
            kn = self.known[op.eng]
            todo = [(sem, cnt) for sem, cnt in need.values() if kn.get(id(sem), 0) < cnt]
            emb = None
            if todo and self.embed and (not op.is_dma) and op.eng in ("pe", "act", "dve"):
                emb = todo.pop()
            for sem, cnt in todo:
                self._wait(op.eng, sem, cnt)
            ins = op.fn(self.eng[op.eng])
            if emb is not None:
                ins._wait_ge(emb[0], emb[1])
                kn[id(emb[0])] = emb[1]
                self.n_waits += 1
            if op.is_dma:
                ent = self.dsem.get(op.semkey)
                if ent is None:
                    ent = [self.nc.semaphore("d_%d" % len(self.dsem)).__enter__(), 0]
                    self.dsem[op.semkey] = ent
                ent[1] += 16
                ins.then_inc(ent[0], 16)
                op.token = (ent[0], ent[1])
            elif op.needed:
                self.ecount[op.eng] += 1
                ins.then_inc(self.esem[op.eng], 1)
                op.token = (self.esem[op.eng], self.ecount[op.eng])
            op.emitted = True
            op.fn = None
        self.pending = []

    def barrier(self):
        self.flush()
        for e in self.ENGS:
            for e2 in self.ENGS:
                if self.ecount[e2] > 0:
                    self._wait(e, self.esem[e2], self.ecount[e2])
            for sem, cnt in self.dsem.values():
                if cnt > 0:
                    self._wait(e, sem, cnt)
        self.last_w = {}
        self.readers = {}
        self.dlast = {}
        self.last_op = {e: None for e in self.ENGS}


def cs(start, n, step):
    return slice(start, start + (n - 1) * step + 1, step)


def alibi_slopes():
    s = [2.0 ** (-8.0 * (i + 1) / 32.0) for i in range(32)]
    return s[0::2], s[1::2]


def build_program(debug=False, phases="ABCDE"):
    nc = bass.Bass("TRN2", target_bir_lowering=False)
    T = Tracker(nc)
    skind = "ExternalOutput" if debug else "Internal"

    def din(name, shape, dt=F32):
        return nc.dram_tensor(name, shape, dt, kind="ExternalInput").ap()

    x_ext = din("x_ext", [T_EXT, D])
    segq_d = din("segq", [3, T_EXT])
    segk_d = din("segk", [3, T_EXT])
    g1_d = din("g1b", [128, D])
    g2_d = din("g2b", [128, D])
    gf_d = din("gfb", [128, D])
    gab_d = din("gab", [128, 16])
    sink_d = din("sinkc", [128, 16])
    w_in_d = din("w_in", [D, IN_W])
    w_out_d = din("w_out", [D, D])
    w_pq_d = din("w_pq", [D, D])
    sk_d = din("sub_keys", [16 * 128, 128])
    eu_d = din("expert_u", [16384, D])
    ev_d = din("expert_v", [16384, D])
    y_d = nc.dram_tensor("y", [T_OWN, D], F32, kind="ExternalOutput").ap()
    qkvT_d = nc.dram_tensor("qkvT", [IN_W, T_EXT], BF16, kind=skind).ap()
    oaT_d = nc.dram_tensor("oaT", [D, T_OWN], F32, kind=skind).ap()
    rst_d = nc.dram_tensor("rst", [2, 128, T_OWN], F32, kind=skind).ap()
    x1_d = nc.dram_tensor("x1", [T_OWN, D], F32, kind=skind).ap()
    scr_d = nc.dram_tensor("scr", [T_OWN, D], F32, kind="Internal").ap()
    eub_d = nc.dram_tensor("eub", [16384, D], BF16, kind="Internal").ap()
    evb_d = nc.dram_tensor("evb", [16384, D], BF16, kind="Internal").ap()
    if debug:
        dbg_d = nc.dram_tensor("dbg", [NT_OWN, 128, 512], F32, kind="ExternalOutput").ap()

    gs = ExitStack()

    def gsb(name, shape, dt=F32):
        return gs.enter_context(nc.sbuf_tensor(name, shape, dt))

    PSB = [gs.enter_context(nc.psum_tensor("psb%d" % i, [128, 1024], BF16)) for i in range(2)]
    PSF = [gs.enter_context(nc.psum_tensor("psf%d" % i, [128, 512], F32)) for i in range(6)]

    ident = gsb("ident", [128, 128], BF16)
    ones128 = gsb("ones128", [128, 128], BF16)
    onesA = gsb("onesA", [128, 128], BF16)
    onesB = gsb("onesB", [128, 128], BF16)
    epsc = gsb("epsc", [128, 1], F32)
    with ExitStack() as es:
        idi = es.enter_context(nc.sbuf_tensor("idi", [128, 128], I32))
        idf = es.enter_context(nc.sbuf_tensor("idf", [128, 128], F32))
        T.op("pool", lambda e: e.iota(idi[:], [[1, 128]], base=0, channel_multiplier=-1), writes=["idi"])
        T.op("dve", lambda e: e.tensor_copy(out=idf[:], in_=idi[:]), reads=["idi"], writes=["idf"])
        T.op("dve", lambda e: e.tensor_single_scalar(out=ident[:], in_=idf[:], scalar=0.0, op=ALU.is_equal),
             reads=["idf"], writes=["ident"])
        T.op("pool", lambda e: e.memset(ones128[:], 1.0), writes=["ones128"])
        T.op("pool", lambda e: e.memset(onesA[:], 0.0), writes=["onesA"])
        T.op("pool", lambda e: e.memset(onesA[:, 0:64], 1.0), reads=["onesA"], writes=["onesA"])
        T.op("pool", lambda e: e.memset(onesB[:], 0.0), writes=["onesB"])
        T.op("pool", lambda e: e.memset(onesB[:, 64:128], 1.0), reads=["onesB"], writes=["onesB"])
        T.op("pool", lambda e: e.memset(epsc[:], EPS), writes=["epsc"])
        T.barrier()

    def rms_scale(xi, xkey, ssq_col, rs_col, key, nfeat, junk):
        T.op("act", lambda e: e.activation(out=junk[:], in_=xi[:], func=AF.Square, accum_out=ssq_col),
             reads=[xkey, key + "z"], writes=[key + "s"])
        T.op("act", lambda e: e.activation(out=rs_col, in_=ssq_col, func=AF.Sqrt, scale=1.0 / nfeat, bias=epsc[:, 0:1]),
             reads=[key + "s"], writes=[key + "r"])
        T.op("dve", lambda e: e.reciprocal(out=rs_col, in_=rs_col), reads=[key + "r"], writes=[key + "r"])

    def transpose16(src, srckey, dst_fn, dstkey):
        for half in range(2):
            pb = PSB[half]
            for k8 in range(8):
                k = half * 8 + k8
                T.op("pe", lambda e, pb=pb, k8=k8, k=k: e.transpose(pb[:, k8 * 128:(k8 + 1) * 128],
                                                                   in_=src[:, k * 128:(k + 1) * 128], identity=ident[:]),
                     reads=[srckey], writes=["psb%d" % half])
            dst = dst_fn(half)
            if half == 0:
                T.op("act", lambda e, pb=pb, dst=dst: e.activation(out=dst, in_=pb[:].rearrange("p (k t) -> p k t", k=8), func=AF.Copy),
                     reads=["psb%d" % half], writes=[dstkey])
            else:
                T.op("dve", lambda e, pb=pb, dst=dst: e.tensor_copy(out=dst, in_=pb[:].rearrange("p (k t) -> p k t", k=8)),
                     reads=["psb%d" % half], writes=[dstkey])

    def phase_A():
        with ExitStack() as es:
            def sb(name, shape, dt=F32):
                return es.enter_context(nc.sbuf_tensor(name, shape, dt))
            xnT = sb("xnT", [128, KC, SBT], BF16)
            xin = [sb("xinA%d" % i, [128, D], F32) for i in range(2)]
            xnb = [sb("xnbA%d" % i, [128, D], BF16) for i in range(2)]
            junk = sb("junkA", [128, D], BF16)
            ssq = sb("ssqA1", [128, NT_EXT])
            rstd = sb("rstdA1", [128, NT_EXT])
            wb = [sb("wbA%d" % i, [128, KC, 128], BF16) for i in range(3)]
            stg = [sb("stgA%d" % i, [128, SBT], BF16) for i in range(2)]
            g1 = sb("g1A", [128, D])
            T.dma("sp", lambda e: e.dma_start(out=g1[:], in_=g1_d[:, :]), "ldg", writes=["g1"])
            T.op("pool", lambda e: e.memset(ssq[:], 0.0), writes=["ssqz"])
            win = w_in_d.rearrange("(kc p) c -> p kc c", p=128)
            for sbi in range(2):
                for tl in range(SBT // 128):
                    t = sbi * (SBT // 128) + tl
                    xi = xin[t % 2]
                    xb = xnb[t % 2]
                    xk = "xin%d" % (t % 2)
                    bk = "xnb%d" % (t % 2)
                    T.dma("sp", lambda e, xi=xi, t=t: e.dma_start(out=xi[:], in_=x_ext[t * 128:(t + 1) * 128, :]),
                          xk, writes=[xk])
                    T.op("act", lambda e, xi=xi, t=t: e.activation(out=junk[:], in_=xi[:], func=AF.Square, accum_out=ssq[:, t:t + 1]),
                         reads=[xk, "ssqz"], writes=["ssq%d" % t])
                    T.op("act", lambda e, t=t: e.activation(out=rstd[:, t:t + 1], in_=ssq[:, t:t + 1], func=AF.Sqrt,
                                                            scale=1.0 / D, bias=epsc[:, 0:1]),
                         reads=["ssq%d" % t], writes=["rstd%d" % t])
                    T.op("dve", lambda e, t=t: e.reciprocal(out=rstd[:, t:t + 1], in_=rstd[:, t:t + 1]),
                         reads=["rstd%d" % t], writes=["rstd%d" % t])
                    T.op("dve", lambda e, xi=xi, xb=xb, t=t: e.scalar_tensor_tensor(out=xb[:], in0=xi[:], scalar=rstd[:, t:t + 1], in1=g1[:],
                                                                                    op0=ALU.mult, op1=ALU.mult),
                         reads=[xk, "rstd%d" % t, "g1"], writes=[bk])
                    transpose16(xb, bk, lambda half, tl=tl: xnT[:, half * 8:half * 8 + 8, tl * 128:(tl + 1) * 128], "xnT%d" % tl)
                for oc in range(NOC):
                    w = wb[oc % 3]
                    wk = "wb%d" % (oc % 3)
                    T.dma("pool", lambda e, w=w, oc=oc: e.dma_start(out=w[:], in_=win[:, :, oc * 128:(oc + 1) * 128]), wk, writes=[wk])
                    st = stg[oc % 2]
                    skeys = []
                    is_q = (oc < 8) or (24 <= oc < 32)
                    tgs = list(range(SBT // 512))
                    if is_q:
                        tgs = [2, 3, 4] if sbi == 0 else [0, 1, 2]
                    for tg in tgs:
                        bi = (oc * (SBT // 512) + tg) % 6
                        acc = PSF[bi]
                        xkeys = ["xnT%d" % (tg * 4 + i) for i in range(4)]
                        for kc in range(KC):
                            T.op("pe", lambda e, acc=acc, w=w, kc=kc, tg=tg: e.matmul(acc[:, 0:512], lhsT=w[:, kc, :],
                                                                                   rhs=xnT[:, kc, tg * 512:(tg + 1) * 512],
                                                                                   start=(kc == 0), stop=(kc == KC - 1)),
                                 reads=[wk] + xkeys, writes=["psf%d" % bi])
                        sk_ = "stg%d_%d" % (oc % 2, tg)
                        skeys.append(sk_)
                        if tg % 2 == 0:
                            T.op("act", lambda e, acc=acc, st=st, tg=tg: e.activation(out=st[:, tg * 512:(tg + 1) * 512], in_=acc[:, 0:512], func=AF.Copy),
                                 reads=["psf%d" % bi], writes=[sk_])
                        else:
                            T.op("dve", lambda e, acc=acc, st=st, tg=tg: e.tensor_copy(out=st[:, tg * 512:(tg + 1) * 512], in_=acc[:, 0:512]),
                                 reads=["psf%d" % bi], writes=[sk_])
                    c_lo, c_hi = tgs[0] * 512, (tgs[-1] + 1) * 512
                    T.dma("sp", lambda e, st=st, oc=oc, sbi=sbi, c_lo=c_lo, c_hi=c_hi: e.dma_start(
                        out=qkvT_d[oc * 128:(oc + 1) * 128, sbi * SBT + c_lo:sbi * SBT + c_hi], in_=st[:, c_lo:c_hi]),
                          "stgo%d" % (oc % 2), reads=skeys)
            T.barrier()

    def phase_B():
        slopes_a, slopes_b = alibi_slopes()
        with ExitStack() as es:
            def sb(name, shape, dt=F32):
                return es.enter_context(nc.sbuf_tensor(name, shape, dt))
            QT = [[sb("QT%d_%d" % (s_, j), [128, T_OWN], BF16) for j in range(2)] for s_ in range(2)]
            KT = [[sb("KT%d_%d" % (s_, j), [128, T_EXT], BF16) for j in range(2)] for s_ in range(2)]
            VT = [sb("VT%d" % i, [128, T_EXT], BF16) for i in range(2)]
            OD = [sb("OD%d" % j, [128, T_OWN]) for j in range(2)]
            rbc = sb("rbcB", [64, T_OWN])
            sqh = [sb("sqB%d" % j, [64, T_OWN], BF16) for j in range(2)]
            ssq = [sb("ssqB%d" % i, [128, T_OWN]) for i in range(2)]
            onesf = sb("onesfB", [128, 64])
            cvals = (-64, 64, -128, 0, 128)
            AR = sb("ARc", [128, 5, 384])
            MK = sb("MKc", [128, 5, 384])
            ari = sb("ari", [128, 384], I32)
            biasT = sb("biasT", [128, 12, 384])
            tmp = [sb("tmpB%d" % i, [128, 384]) for i in range(4)]
            PT = [sb("PTB%d" % i, [128, 384], BF16) for i in range(4)]
            Vaug = [sb("Vaug%d" % i, [128, 2, 65], BF16) for i in range(2)]
            esink = sb("esink", [128, 16])

            for s_ in range(2):
                for j in range(2):
                    T.dma("pool", lambda e, s_=s_, j=j: e.dma_start(out=QT[s_][j][64:67, :], in_=segq_d[:, HALO:HALO + T_OWN]),
                          "seg%d" % (s_ * 2 + j), writes=["QTseg"])
                    T.dma("pool", lambda e, s_=s_, j=j: e.dma_start(out=KT[s_][j][64:67, :], in_=segk_d[:, :]),
                          "segk%d" % (s_ * 2 + j), writes=["KTseg"])
            conv_jobs = []
            for ti, (src, dst) in enumerate(((eu_d, eub_d), (ev_d, evb_d))):
                sv = src.rearrange("(p r) d -> p r d", p=128)
                dv = dst.rearrange("(p r) d -> p r d", p=128)
                for c in range(32):
                    conv_jobs.append((sv, dv, c))
            conv_state = [0, 0]

            def conv_tick():
                conv_state[1] += 1
                if conv_state[1] % 14 == 0 and conv_state[0] < len(conv_jobs):
                    sv, dv, c = conv_jobs[conv_state[0]]
                    T.dma("pool", lambda e: e.dma_start(out=dv[:, c * 4:(c + 1) * 4, :], in_=sv[:, c * 4:(c + 1) * 4, :]),
                          "cv%d" % (conv_state[0] % 4))
                    conv_state[0] += 1

            T.dma("sp", lambda e: e.dma_start(out=esink[:], in_=sink_d[:, :]), "sink", writes=["esink"])
            T.op("act", lambda e: e.activation(out=esink[:], in_=esink[:], func=AF.Exp), reads=["esink"], writes=["esink"])
            T.op("pool", lambda e: e.memset(onesf[:], 1.0), writes=["onesf"])
            for i in range(2):
                T.op("pool", lambda e, i=i: e.memset(Vaug[i][:], 1.0), writes=["Vaug%d" % i])
                T.op("pool", lambda e, i=i: e.memset(ssq[i][:], 0.0), writes=["ssqB%d" % i])
            for ci, c in enumerate(cvals):
                W = 64.0 if ci < 2 else 128.0
                T.op("pool", lambda e, c=c: e.iota(ari[:], [[-1, 384]], base=c, channel_multiplier=1), reads=["ari"], writes=["ari"])
                T.op("dve", lambda e, ci=ci: e.tensor_copy(out=AR[:, ci, :], in_=ari[:]), reads=["ari"], writes=["AR%d" % ci])
                T.op("act", lambda e, ci=ci: e.activation(out=AR[:, ci, :], in_=AR[:, ci, :], func=AF.Abs), reads=["AR%d" % ci], writes=["AR%d" % ci])
                T.op("dve", lambda e, ci=ci, W=W: e.tensor_scalar(out=MK[:, ci, :], in0=AR[:, ci, :], scalar1=W, scalar2=NEGM,
                                                                 op0=ALU.is_gt, op1=ALU.mult),
                     reads=["AR%d" % ci], writes=["MK%d" % ci])

            def load_hp(hp):
                slot = hp % 2
                vk = "VT%d" % slot
                if hp < 8:
                    for j in range(2):
                        T.dma("sp", lambda e, j=j: e.dma_start(out=QT[slot][j][0:64, :], in_=qkvT_d[hp * 128 + 64 * j:hp * 128 + 64 * j + 64, HALO:HALO + T_OWN]),
                              "QT%d_%d" % (slot, j), writes=["QT%d_%d" % (slot, j)])
                        T.dma("sp", lambda e, j=j: e.dma_start(out=KT[slot][j][0:64, :], in_=qkvT_d[(8 + hp) * 128 + 64 * j:(8 + hp) * 128 + 64 * j + 64, :]),
                              "KT%d_%d" % (slot, j), writes=["KT%d_%d" % (slot, j)])
                    T.dma("sp", lambda e: e.dma_start(out=VT[slot][:], in_=qkvT_d[(16 + hp) * 128:(17 + hp) * 128, :]), vk, writes=[vk])
                else:
                    hb = hp - 8
                    kv = hb // 2
                    kr = 4096 + 64 * kv
                    vr = 4352 + 64 * kv
                    for j in range(2):
                        T.dma("sp", lambda e, j=j: e.dma_start(out=QT[slot][j][0:64, :], in_=qkvT_d[(24 + hb) * 128 + 64 * j:(24 + hb) * 128 + 64 * j + 64, HALO:HALO + T_OWN]),
                              "QT%d_%d" % (slot, j), writes=["QT%d_%d" % (slot, j)])
                        T.dma("sp", lambda e, j=j: e.dma_start(out=KT[slot][j][0:64, :], in_=qkvT_d[kr:kr + 64, :]),
                              "KT%d_%d" % (slot, j), writes=["KT%d_%d" % (slot, j)])
                        T.dma("sp", lambda e, j=j: e.dma_start(out=VT[slot][64 * j:64 * j + 64, :], in_=qkvT_d[vr:vr + 64, :]), vk + "_%d" % j, writes=[vk])

            it = 0
            load_hp(0)
            for hp in range(16):
                isB = hp >= 8
                slot = hp % 2
                vk = "VT%d" % slot
                if hp + 1 < 16:
                    load_hp(hp + 1)
                if not isB:
                    pats = [(1, 64, 0, 1), (4, 64, 0, 1), (16, 64, 0, 1)]
                    slopes = [slopes_a[2 * hp], slopes_a[2 * hp + 1]]
                    heads = [2 * hp, 2 * hp + 1]
                else:
                    hb = hp - 8
                    pats = [(1, 128, 2, 3)]
                    slopes = [slopes_b[2 * hb], slopes_b[2 * hb + 1]]
                    heads = [2 * hb, 2 * hb + 1]
                bidx = {}
                nb = 0
                for j in range(2):
                    for pi, (d, W, c0, c1) in enumerate(pats):
                        cis = (c0, c1) if not isB else (2, 3, 4)
                        for ci in cis:
                            bidx[(j, pi, ci)] = nb
                            T.op("dve", lambda e, nb=nb, ci=ci, sl=-slopes[j] * d: e.scalar_tensor_tensor(
                                out=biasT[:, nb, :], in0=AR[:, ci, :], scalar=sl, in1=MK[:, ci, :], op0=ALU.mult, op1=ALU.add),
                                 reads=["AR%d" % ci, "MK%d" % ci], writes=["bias%d" % nb])
                            nb += 1
                for j in range(2):
                    T.op("pool", lambda e, j=j: e.memset(OD[j][:], 0.0), writes=["OD%d" % j])
                iters = []
                for pi, (d, W, c0, c1) in enumerate(pats):
                    ua = HALO // d
                    L = T_OWN // d
                    ub = ua + L
                    nkt = -(-(L + 2 * W) // 128)
                    for r in range(d):
                        for m in range(nkt):
                            k0 = ua - W + 128 * m
                            nk = min(128, ub + W - k0)
                            qlo = max(ua, k0 - W)
                            qhi = min(ub, k0 + nk + W)
                            nq = qhi - qlo
                            ci = cvals.index(k0 - qlo)
                            iters.append((pi, nk, nq, ci, cs(r + d * k0, nk, d), cs(r + d * qlo - HALO, nq, d)))

                def stage1(itn, desc, slot=slot, vk=vk, bidx=bidx):
                    pi, nk, nq, ci, kcols, qcols = desc
                    pb = PSB[itn % 2]
                    pbk = "psb%d" % (itn % 2)
                    va = Vaug[itn % 2]
                    vak = "Vaug%d" % (itn % 2)
                    T.op("pe", lambda e: e.transpose(pb[0:nk, 0:128], in_=VT[slot][:, kcols], identity=ident[:]),
                         reads=[vk], writes=[pbk])
                    T.op("act", lambda e: e.activation(out=va[0:nk, :, 0:64], in_=pb[0:nk, 0:128].rearrange("p (a b) -> p a b", b=64), func=AF.Copy),
                         reads=[pbk], writes=[vak])
                    for j in range(2):
                        si = (2 * itn + j) % 4
                        S = PSF[si]
                        sk_ = "psf%d" % si
                        tm = tmp[si]
                        pt = PT[si]
                        T.op("pe", lambda e, S=S, j=j: e.matmul(
                            S[0:nk, 0:nq], lhsT=KT[slot][j][0:67, kcols], rhs=QT[slot][j][0:67, qcols], start=True, stop=True),
                             reads=["KT%d_%d" % (slot, j), "QT%d_%d" % (slot, j), "KTseg", "QTseg"], writes=[sk_])
                        bi = bidx[(j, pi, ci)]
                        T.op("dve", lambda e, S=S, tm=tm, bi=bi: e.scalar_tensor_tensor(
                            out=tm[0:nk, 0:nq], in0=S[0:nk, 0:nq], scalar=0.125, in1=biasT[0:nk, bi, 0:nq], op0=ALU.mult, op1=ALU.add),
                             reads=[sk_, "bias%d" % bi], writes=["tmp%d" % si])
                        T.op("act", lambda e, tm=tm, pt=pt: e.activation(out=pt[0:nk, 0:nq], in_=tm[0:nk, 0:nq], func=AF.Exp),
                             reads=["tmp%d" % si], writes=["PT%d" % si])

                def stage2(itn, desc):
                    pi, nk, nq, ci, kcols, qcols = desc
                    va = Vaug[itn % 2]
                    vak = "Vaug%d" % (itn % 2)
                    for j in range(2):
                        si = (2 * itn + j) % 4
                        pt = PT[si]
                        Ops = PSF[4 + j]
                        opk = "psf%d" % (4 + j)
                        T.op("pe", lambda e, j=j, pt=pt, Ops=Ops: e.matmul(Ops[0:65, 0:nq], lhsT=va[0:nk, j, :], rhs=pt[0:nk, 0:nq], start=True, stop=True),
                             reads=[vak, "PT%d" % si], writes=[opk])
                        T.op("dve", lambda e, j=j, Ops=Ops: e.tensor_tensor(out=OD[j][0:65, qcols], in0=Ops[0:65, 0:nq], in1=OD[j][0:65, qcols], op=ALU.add),
                             reads=[opk, "OD%d" % j], writes=["OD%d" % j])

                stage1(it, iters[0])
                for i_ in range(len(iters)):
                    if i_ + 1 < len(iters):
                        stage1(it + i_ + 1, iters[i_ + 1])
                    stage2(it + i_, iters[i_])
                    conv_tick()
                it += len(iters)
                sidx = 1 if isB else 0
                for j in range(2):
                    odk = "OD%d" % j
                    if isB:
                        hh = heads[j]
                        T.op("dve", lambda e, j=j, hh=hh: e.tensor_scalar(out=OD[j][64:65, :], in0=OD[j][64:65, :], scalar1=esink[64:65, hh:hh + 1],
                                                                         scalar2=None, op0=ALU.add),
                             reads=[odk, "esink"], writes=[odk])
                    for g in range(T_OWN // 512):
                        pp = PSF[g % 4]
                        ppk = "psf%d" % (g % 4)
                        T.op("pe", lambda e, j=j, pp=pp, g=g: e.matmul(pp[0:64, 0:512], lhsT=onesf[64:65, 0:64], rhs=OD[j][64:65, g * 512:(g + 1) * 512],
                                                                      start=True, stop=True),
                             reads=[odk, "onesf"], writes=[ppk])
                        T.op("dve", lambda e, pp=pp, g=g: e.reciprocal(out=rbc[:, g * 512:(g + 1) * 512], in_=pp[0:64, 0:512]),
                             reads=[ppk], writes=["rbc%d" % g])
                    rkeys = ["rbc%d" % g for g in range(T_OWN // 512)]
                    T.op("dve", lambda e, j=j: e.tensor_tensor(out=OD[j][0:64, :], in0=OD[j][0:64, :], in1=rbc[:, :], op=ALU.mult),
                         reads=[odk] + rkeys, writes=[odk])
                    T.op("act", lambda e, j=j: e.activation(out=sqh[j][:], in_=OD[j][0:64, :], func=AF.Square), reads=[odk], writes=["sq%d" % j])
                    T.dma("sp", lambda e, j=j, hp=hp: e.dma_start(out=oaT_d[hp * 128 + 64 * j:hp * 128 + 64 * j + 64, :], in_=OD[j][0:64, :]), "oaT%d" % j, reads=[odk])
                for g in range(T_OWN // 512):
                    pp = PSF[4 + g % 2]
                    ppk = "psf%d" % (4 + g % 2)
                    for j in range(2):
                        T.op("pe", lambda e, pp=pp, g=g, j=j: e.matmul(pp[:, 0:512], lhsT=ones128[0:64, :], rhs=sqh[j][:, g * 512:(g + 1) * 512],
                                                                      start=(j == 0), stop=(j == 1)),
                             reads=["sq%d" % j], writes=[ppk])
                    T.op("dve", lambda e, pp=pp, g=g, sidx=sidx: e.tensor_tensor(out=ssq[sidx][:, g * 512:(g + 1) * 512], in0=pp[:, 0:512],
                                                                                 in1=ssq[sidx][:, g * 512:(g + 1) * 512], op=ALU.add),
                         reads=[ppk, "ssqB%d" % sidx], writes=["ssqB%d" % sidx])
            while conv_state[0] < len(conv_jobs):
                conv_state[1] = 13
                conv_tick()
            for i in range(2):
                T.op("act", lambda e, i=i: e.activation(out=ssq[i][:], in_=ssq[i][:], func=AF.Sqrt, scale=1.0 / 1024.0, bias=epsc[:, 0:1]),
                     reads=["ssqB%d" % i], writes=["ssqB%d" % i])
                T.op("dve", lambda e, i=i: e.reciprocal(out=ssq[i][:], in_=ssq[i][:]), reads=["ssqB%d" % i], writes=["ssqB%d" % i])
                T.dma("sp", lambda e, i=i: e.dma_start(out=rst_d[i], in_=ssq[i][:]), "rst%d" % i, reads=["ssqB%d" % i])
            T.barrier()

    def phase_C():
        with ExitStack() as es:
            def sb(name, shape, dt=F32):
                return es.enter_context(nc.sbuf_tensor(name, shape, dt))
            wout = sb("wout", [128, KC, D], BF16)
            rst = sb("rstC", [128, 2, T_OWN])
            gab = sb("gabC", [128, 16])
            ld = sb("ldC", [128, KC, 512])
            mixedT = [sb("mixT%d" % i, [128, KC, 512], BF16) for i in range(2)]
            xin = [sb("xinC%d" % i, [128, D]) for i in range(2)]
            x1t = [sb("x1tC%d" % i, [128, D]) for i in range(2)]
            for kc in range(KC):
                for h in range(2):
                    T.dma("pool", lambda e, kc=kc, h=h: e.dma_start(out=wout[:, kc, h * 1024:(h + 1) * 1024],
                                                                      in_=w_out_d[kc * 128:(kc + 1) * 128, h * 1024:(h + 1) * 1024]),
                          "wo%d" % ((2 * kc + h) % 4), writes=["wout%d_%d" % (kc, h)])
            woutkeys = ["wout%d_%d" % (kc, h) for kc in range(KC) for h in range(2)]
            for i in range(2):
                T.dma("sp", lambda e, i=i: e.dma_start(out=rst[:, i, :], in_=rst_d[i]), "rstl%d" % i, writes=["rst"])
            T.dma("sp", lambda e: e.dma_start(out=gab[:], in_=gab_d[:, :]), "gab", writes=["gab"])
            oaT_v = oaT_d.rearrange("(c p) t -> p c t", p=128)
            T.dma("sp", lambda e: e.dma_start(out=ld[:], in_=oaT_v[:, :, 0:512]), "ldC", writes=["ld"])
            for tg in range(T_OWN // 512):
                mx = mixedT[tg % 2]
                mk_ = "mix%d" % (tg % 2)
                for c in range(16):
                    T.op("dve", lambda e, mx=mx, c=c, tg=tg: e.scalar_tensor_tensor(out=mx[:, c, :], in0=ld[:, c, :], scalar=gab[:, c:c + 1],
                                                                                  in1=rst[:, c // 8, tg * 512:(tg + 1) * 512], op0=ALU.mult, op1=ALU.mult),
                         reads=["ld", "gab", "rst"], writes=[mk_ + "_%d" % c])
                mkeys = [mk_ + "_%d" % c for c in range(16)]
                if tg + 1 < T_OWN // 512:
                    T.dma("sp", lambda e, tg=tg: e.dma_start(out=ld[:], in_=oaT_v[:, :, (tg + 1) * 512:(tg + 2) * 512]), "ldC", writes=["ld"])
                for tl in range(4):
                    t = tg * 4 + tl
                    xi = xin[t % 2]
                    xk = "xinC%d" % (t % 2)
                    xo = x1t[t % 2]
                    xok = "x1t%d" % (t % 2)
                    T.dma("sp", lambda e, xi=xi, t=t: e.dma_start(out=xi[:], in_=x_ext[HALO + t * 128:HALO + (t + 1) * 128, :]), xk, writes=[xk])
                    okeys = []
                    for ng in range(4):
                        bi = (t * 4 + ng) % 6
                        acc = PSF[bi]
                        for kc in range(KC):
                            T.op("pe", lambda e, acc=acc, mx=mx, kc=kc, tl=tl, ng=ng: e.matmul(acc[:, 0:512], lhsT=mx[:, kc, tl * 128:(tl + 1) * 128],
                                                                                           rhs=wout[:, kc, ng * 512:(ng + 1) * 512],
                                                                                           start=(kc == 0), stop=(kc == KC - 1)),
                                 reads=mkeys + woutkeys, writes=["psf%d" % bi])
                        T.op("dve", lambda e, acc=acc, xo=xo, xi=xi, ng=ng: e.tensor_tensor(out=xo[:, ng * 512:(ng + 1) * 512], in0=acc[:, 0:512],
                                                                                          in1=xi[:, ng * 512:(ng + 1) * 512], op=ALU.add),
                             reads=["psf%d" % bi, xk], writes=[xok + "_%d" % ng])
                        okeys.append(xok + "_%d" % ng)
                    T.dma("pool", lambda e, xo=xo, t=t: e.dma_start(out=x1_d[t * 128:(t + 1) * 128, :], in_=xo[:]), "x1o%d" % (t % 2), reads=okeys)
            T.barrier()

    def phase_DE(do_E=True):
        with ExitStack() as es_o:
            def osb(name, shape, dt=F32):
                return es_o.enter_context(nc.sbuf_tensor(name, shape, dt))
            eid2 = osb("eid2", [128, 3, 128], I32)
            gat2 = osb("gat2", [128, 3, 128], F32)
            x1j = osb("x1jD", [128, D], F32)
            g2 = osb("g2D", [128, D], F32)
            x1i = osb("x1iD", [128, D], F32)
            ssq = osb("ssqD", [128, 3 * NT_OWN], F32)
            rs = osb("rsD", [128, 3 * NT_OWN], F32)
            junk = osb("junkD", [128, D], BF16)
            T.dma("sp", lambda e: e.dma_start(out=g2[:], in_=g2_d[:, :]), "g2", writes=["g2"])
            T.op("pool", lambda e: e.memset(ssq[:], 0.0), writes=["ssqDz"])
            with ExitStack() as es:
                def sb(name, shape, dt=F32):
                    return es.enter_context(nc.sbuf_tensor(name, shape, dt))
                wpq = sb("wpq", [128, KC, D], BF16)
                skb = sb("skb", [128, 16, 128], BF16)
                skT = sb("skT", [128, 16, 128], BF16)
                hnb = sb("hnbD", [128, D], BF16)
                hnT = sb("hnTD", [128, KC, 512], BF16)
                qT = sb("qTD", [128, 16, 512], BF16)
                scs = [sb("scsD%d" % i, [128, D]) for i in range(2)]
                for kc in range(KC):
                    for h in range(2):
                        T.dma("pool", lambda e, kc=kc, h=h: e.dma_start(out=wpq[:, kc, h * 1024:(h + 1) * 1024],
                                                                          in_=w_pq_d[kc * 128:(kc + 1) * 128, h * 1024:(h + 1) * 1024]),
                              "wq%d" % ((2 * kc + h) % 4), writes=["wpq%d_%d" % (kc, h)])
                wpqkeys = ["wpq%d_%d" % (kc, h) for kc in range(KC) for h in range(2)]
                T.dma("pool", lambda e: e.dma_start(out=skb[:], in_=sk_d.rearrange("(c n) d -> n c d", n=128)), "skb", writes=["skb"])
                for half in range(2):
                    pb = PSB[half]
                    for k8 in range(8):
                        c = half * 8 + k8
                        T.op("pe", lambda e, pb=pb, k8=k8, c=c: e.transpose(pb[:, k8 * 128:(k8 + 1) * 128], in_=skb[:, c, :], identity=ident[:]),
                             reads=["skb"], writes=["psb%d" % half])
                    T.op("dve", lambda e, pb=pb, half=half: e.tensor_copy(out=skT[:, half * 8:half * 8 + 8, :], in_=pb[:].rearrange("p (k t) -> p k t", k=8)),
                         reads=["psb%d" % half], writes=["skT%d" % half])
                for tgp in range(NT_OWN // 4):
                    for tl in range(4):
                        t = tgp * 4 + tl
                        xs = x1i if t % 2 == 0 else x1j
                        xk = "x1d%d" % (t % 2)
                        T.dma("sp", lambda e, t=t, xs=xs: e.dma_start(out=xs[:], in_=x1_d[t * 128:(t + 1) * 128, :]), xk, writes=[xk])
                        T.op("act", lambda e, t=t, xs=xs: e.activation(out=junk[:], in_=xs[:], func=AF.Square, accum_out=ssq[:, t:t + 1]),
                             reads=[xk, "ssqDz"], writes=["ssqD%d" % t])
                        T.op("act", lambda e, t=t: e.activation(out=rs[:, t:t + 1], in_=ssq[:, t:t + 1], func=AF.Sqrt, scale=1.0 / D, bias=epsc[:, 0:1]),
                             reads=["ssqD%d" % t], writes=["rsD%d" % t])
                        T.op("dve", lambda e, t=t: e.reciprocal(out=rs[:, t:t + 1], in_=rs[:, t:t + 1]), reads=["rsD%d" % t], writes=["rsD%d" % t])
                        T.op("dve", lambda e, t=t, xs=xs: e.scalar_tensor_tensor(out=hnb[:], in0=xs[:], scalar=rs[:, t:t + 1], in1=g2[:], op0=ALU.mult, op1=ALU.mult),
                             reads=[xk, "rsD%d" % t, "g2"], writes=["hnb"])
                        transpose16(hnb, "hnb", lambda half, tl=tl: hnT[:, half * 8:half * 8 + 8, tl * 128:(tl + 1) * 128], "hnT%d" % tl)
                    hkeys = ["hnT%d" % tl for tl in range(4)]
                    for c in range(16):
                        bi = c % 4
                        acc = PSF[bi]
                        for kc in range(KC):
                            T.op("pe", lambda e, acc=acc, c=c, kc=kc: e.matmul(acc[:, 0:512], lhsT=wpq[:, kc, c * 128:(c + 1) * 128],
                                                                             rhs=hnT[:, kc, :], start=(kc == 0), stop=(kc == KC - 1)),
                                 reads=wpqkeys + hkeys, writes=["psf%d" % bi])
                        if c % 2 == 0:
                            T.op("act", lambda e, acc=acc, c=c: e.activation(out=qT[:, c, :], in_=acc[:, 0:512], func=AF.Copy),
                                 reads=["psf%d" % bi], writes=["qT%d" % c])
                        else:
                            T.op("dve", lambda e, acc=acc, c=c: e.tensor_copy(out=qT[:, c, :], in_=acc[:, 0:512]),
                                 reads=["psf%d" % bi], writes=["qT%d" % c])
                    for tl in range(4):
                        t = tgp * 4 + tl
                        sc_ = scs[t % 2]
                        sck = "scs%d" % (t % 2)
                        okeys = []
                        for cg in range(4):
                            bi = 4 + cg % 2
                            acc = PSF[bi]
                            for ci in range(4):
                                c = cg * 4 + ci
                                T.op("pe", lambda e, acc=acc, ci=ci, c=c, tl=tl: e.matmul(acc[:, ci * 128:(ci + 1) * 128], lhsT=qT[:, c, tl * 128:(tl + 1) * 128],
                                                                                         rhs=skT[:, c, :], start=True, stop=True),
                                     reads=["qT%d" % c, "skT0", "skT1"], writes=["psf%d" % bi])
                            if cg % 2 == 0:
                                T.op("act", lambda e, acc=acc, cg=cg, sc_=sc_: e.activation(out=sc_[:, cg * 512:(cg + 1) * 512], in_=acc[:, 0:512], func=AF.Copy),
                                     reads=["psf%d" % bi], writes=[sck + "_%d" % cg])
                            else:
                                T.op("dve", lambda e, acc=acc, cg=cg, sc_=sc_: e.tensor_copy(out=sc_[:, cg * 512:(cg + 1) * 512], in_=acc[:, 0:512]),
                                     reads=["psf%d" % bi], writes=[sck + "_%d" % cg])
                            okeys.append(sck + "_%d" % cg)
                        T.dma("pool", lambda e, t=t, sc_=sc_: e.dma_start(out=scr_d[t * 128:(t + 1) * 128, :], in_=sc_[:]), "scso%d" % (t % 2), reads=okeys)
                T.barrier()
            if not do_E:
                return
            with ExitStack() as es:
                def sb(name, shape, dt=F32):
                    return es.enter_context(nc.sbuf_tensor(name, shape, dt))
                W1 = sb("W1", [128, D])
                W2 = sb("W2", [128, D])
                W3 = sb("W3", [128, D])
                W4 = sb("W4", [128, D])
                v16 = sb("v16", [128, 16, 16])
                i16 = sb("i16", [128, 16, 16], U32)
                i16f = sb("i16f", [128, 16, 16])
                top = sb("topD", [128, 8, 16])
                pos = sb("posD", [128, 8, 16], U32)
                posf = sb("posfD", [128, 8, 16])
                af_ = sb("afD", [128, 8, 16])
                bf_ = sb("bfD", [128, 8, 16])
                s1 = sb("s1D", [128, 8, 16])
                s2 = sb("s2D", [128, 8, 16])
                eg = sb("egD", [128, 8, 16])
                esum = sb("esumD", [128, 8])
                io16i = sb("io16i", [128, 16], I32)
                io16 = sb("io16", [128, 16])
                thr15 = sb("thr15", [128, 15])
                NS = 16
                rows = [sb("rowE%d" % i, [128, D], BF16) for i in range(NS)]
                ND = 4
                diag = [sb("diagE%d" % i, [128, 128], BF16) for i in range(ND)]
                gf = sb("gfE", [128, D])
                hnb = sb("hnbE", [128, D], BF16)
                yacc = sb("yaccE", [128, D])
                dots = sb("dotsE", [128, 128])
                wgt = sb("wgtE", [128, 128])
                junkf = sb("junkfE", [128, D], BF16)
                junka = sb("junkaE", [128, D], BF16)
                NP = 4
                prod = [sb("prodE%d" % i, [128, D], BF16) for i in range(NP)]
                T.dma("sp", lambda e: e.dma_start(out=gf[:], in_=gf_d[:, :]), "gf", writes=["gf"])
                T.op("pool", lambda e: e.iota(io16i[:], [[1, 16]], base=0, channel_multiplier=0), writes=["io16i"])
                T.op("dve", lambda e: e.tensor_copy(out=io16[:], in_=io16i[:]), reads=["io16i"], writes=["io16"])
                T.op("dve", lambda e: e.tensor_scalar(out=thr15[:], in0=io16[:, 0:15], scalar1=16.0, scalar2=16.0, op0=ALU.mult, op1=ALU.add),
                     reads=["io16"], writes=["thr15"])

                def topk(t):
                    sl = t % 3
                    T.dma("sp", lambda e: e.dma_start(out=W1[:], in_=scr_d[t * 128:(t + 1) * 128, :]), "scl", writes=["W1"])
                    sc = W1[:].rearrange("p (c n) -> p c n", c=16)
                    sc2 = W2[:].rearrange("p (c n) -> p c n", c=16)
                    for c in range(16):
                        T.op("dve", lambda e, c=c: e.max(out=v16[:, c, 0:8], in_=sc[:, c, :]), reads=["W1"], writes=["v16"])
                        T.op("dve", lambda e, c=c: e.max_index(out=i16[:, c, 0:8], in_max=v16[:, c, 0:8], in_values=sc[:, c, :]), reads=["W1", "v16"], writes=["i16"])
                        T.op("dve", lambda e, c=c: e.match_replace(out=sc2[:, c, :], in_to_replace=v16[:, c, 0:8], in_values=sc[:, c, :], imm_value=-1e30),
                             reads=["W1", "v16"], writes=["W2"])
                        T.op("dve", lambda e, c=c: e.max(out=v16[:, c, 8:16], in_=sc2[:, c, :]), reads=["W2", "v16"], writes=["v16"])
                        T.op("dve", lambda e, c=c: e.max_index(out=i16[:, c, 8:16], in_max=v16[:, c, 8:16], in_values=sc2[:, c, :]), reads=["W2", "v16", "i16"], writes=["i16"])
                    v4 = v16[:].rearrange("p (h s) k -> p h s k", s=2)
                    cand = W3[:].rearrange("p (h a b) -> p h a b", h=8, a=16)
                    T.op("dve", lambda e: e.tensor_tensor(out=cand, in0=v4[:, :, 0, :].unsqueeze(3).to_broadcast([128, 8, 16, 16]),
                                                          in1=v4[:, :, 1, :].unsqueeze(2).to_broadcast([128, 8, 16, 16]), op=ALU.add),
                         reads=["v16"], writes=["W3"])
                    c3 = W3[:].rearrange("p (h n) -> p h n", h=8)
                    c4 = W4[:].rearrange("p (h n) -> p h n", h=8)
                    for h in range(8):
                        T.op("dve", lambda e, h=h: e.max(out=top[:, h, 0:8], in_=c3[:, h, :]), reads=["W3"], writes=["top"])
                        T.op("dve", lambda e, h=h: e.max_index(out=pos[:, h, 0:8], in_max=top[:, h, 0:8], in_values=c3[:, h, :]), reads=["W3", "top"], writes=["pos"])
                        T.op("dve", lambda e, h=h: e.match_replace(out=c4[:, h, :], in_to_replace=top[:, h, 0:8], in_values=c3[:, h, :], imm_value=-1e30),
                             reads=["W3", "top"], writes=["W4"])
                        T.op("dve", lambda e, h=h: e.max(out=top[:, h, 8:16], in_=c4[:, h, :]), reads=["W4", "top"], writes=["top"])
                        T.op("dve", lambda e, h=h: e.max_index(out=pos[:, h, 8:16], in_max=top[:, h, 8:16], in_values=c4[:, h, :]), reads=["W4", "top", "pos"], writes=["pos"])
                    T.op("dve", lambda e: e.tensor_copy(out=posf[:], in_=pos[:]), reads=["pos"], writes=["posf"])
                    T.op("dve", lambda e: e.tensor_copy(out=i16f[:], in_=i16[:]), reads=["i16"], writes=["i16f"])
                    big = W1[:].rearrange("p (h k a) -> p h k a", h=8, k=16)
                    big2 = W2[:].rearrange("p (h k a) -> p h k a", h=8, k=16)
                    T.op("dve", lambda e: e.tensor_tensor(out=big[:, :, :, 0:15], in0=posf[:].unsqueeze(3).to_broadcast([128, 8, 16, 15]),
                                                          in1=thr15[:].unsqueeze(1).unsqueeze(1).to_broadcast([128, 8, 16, 15]), op=ALU.is_ge),
                         reads=["posf", "thr15", "W1"], writes=["W1"])
                    T.op("dve", lambda e: e.tensor_reduce(out=af_[:], in_=big[:, :, :, 0:15], axis=AX.X, op=ALU.add), reads=["W1"], writes=["af"])
                    T.op("dve", lambda e: e.scalar_tensor_tensor(out=bf_[:], in0=af_[:], scalar=-16.0, in1=posf[:], op0=ALU.mult, op1=ALU.add),
                         reads=["af", "posf"], writes=["bf"])
                    i4 = i16f[:].rearrange("p (h s) k -> p h s k", s=2)
                    for (src, side, dst, dk) in ((af_, 0, s1, "s1"), (bf_, 1, s2, "s2")):
                        sk_ = "af" if side == 0 else "bf"
                        T.op("dve", lambda e, src=src: e.tensor_tensor(out=big, in0=src[:].unsqueeze(3).to_broadcast([128, 8, 16, 16]),
                                                                       in1=io16[:].unsqueeze(1).unsqueeze(1).to_broadcast([128, 8, 16, 16]), op=ALU.is_equal),
                             reads=[sk_, "io16", "W1"], writes=["W1"])
                        T.op("dve", lambda e, side=side: e.tensor_tensor(out=big2, in0=big, in1=i4[:, :, side, :].unsqueeze(2).to_broadcast([128, 8, 16, 16]), op=ALU.mult),
                             reads=["W1", "i16f", "W2"], writes=["W2"])
                        T.op("dve", lambda e, dst=dst: e.tensor_reduce(out=dst[:], in_=big2, axis=AX.X, op=ALU.add), reads=["W2"], writes=[dk])
                    T.op("dve", lambda e: e.scalar_tensor_tensor(out=s1[:], in0=s1[:], scalar=128.0, in1=s2[:], op0=ALU.mult, op1=ALU.add),
                         reads=["s1", "s2"], writes=["s1"])
                    T.op("dve", lambda e: e.tensor_copy(out=eid2[:, sl, :], in_=s1[:].rearrange("p h k -> p (h k)")), reads=["s1"], writes=["eid%d" % sl])
                    T.op("dve", lambda e: e.tensor_tensor(out=eg[:], in0=top[:], in1=top[:, :, 0:1].to_broadcast([128, 8, 16]), op=ALU.subtract),
                         reads=["top"], writes=["eg"])
                    T.op("act", lambda e: e.activation(out=eg[:], in_=eg[:], func=AF.Exp), reads=["eg"], writes=["eg"])
                    T.op("dve", lambda e: e.tensor_reduce(out=esum[:], in_=eg[:], axis=AX.X, op=ALU.add), reads=["eg"], writes=["esum"])
                    T.op("dve", lambda e: e.reciprocal(out=esum[:], in_=esum[:]), reads=["esum"], writes=["esum"])
                    T.op("dve", lambda e: e.tensor_tensor(out=gat2[:, sl, :].rearrange("p (h k) -> p h k", h=8), in0=eg[:],
                                                          in1=esum[:].unsqueeze(2).to_broadcast([128, 8, 16]), op=ALU.mult),
                         reads=["eg", "esum"], writes=["gat%d" % sl])
                    if debug:
                        T.dma("sp", lambda e: e.dma_start(out=dbg_d[t, :, 0:128], in_=eid2[:, sl, :].bitcast(F32)), "dbg0", reads=["eid%d" % sl])
                        T.dma("sp", lambda e: e.dma_start(out=dbg_d[t, :, 128:256], in_=gat2[:, sl, :]), "dbg1", reads=["gat%d" % sl])

                dkeys = ["dots%d" % i for i in range(128)]
                ykeys = ["yacc%d" % ng for ng in range(4)]
                x1s = [x1i, x1j]
                cnt = {"gi": 0, "di": 0, "pi": 0}

                def topk_ops(t):
                    saved = (T.op, T.dma)
                    lst = []
                    T.op = lambda *a, **k: lst.append(("op", a, k))
                    T.dma = lambda *a, **k: lst.append(("dma", a, k))
                    try:
                        topk(t)
                    finally:
                        T.op, T.dma = saved
                    return lst

                def play(lst, n):
                    for _ in range(n):
                        if not lst:
                            return
                        kind, a, k = lst.pop(0)
                        (T.op if kind == "op" else T.dma)(*a, **k)

                def load_x1(t):
                    xs = x1s[t % 2]
                    T.dma("sp", lambda e: e.dma_start(out=xs[:], in_=x1_d[t * 128:(t + 1) * 128, :]), "x1s%d" % (t % 2), writes=["x1s%d" % (t % 2)])

                def make_hnb(t):
                    xs = x1s[t % 2]
                    T.op("dve", lambda e: e.scalar_tensor_tensor(out=hnb[:], in0=xs[:], scalar=rs[:, t:t + 1], in1=g2[:], op0=ALU.mult, op1=ALU.mult),
                         reads=["x1s%d" % (t % 2), "g2"], writes=["hnb"])

                def next_row():
                    s_ = cnt["gi"] % NS
                    cnt["gi"] += 1
                    return rows[s_], "row%d" % s_

                def u_row(t, hk):
                    sl = t % 3
                    rw, rk = next_row()
                    T.dma("pool", lambda e: e.indirect_dma_start(
                        out=rw[:], out_offset=None, in_=eub_d[:, :], in_offset=bass.IndirectOffsetOnAxis(ap=eid2[:, sl, hk:hk + 1], axis=0)),
                          rk, reads=["eid%d" % sl], writes=[rk])
                    pr = prod[cnt["pi"] % NP]
                    pk = "prod%d" % (cnt["pi"] % NP)
                    cnt["pi"] += 1
                    T.op("dve", lambda e: e.tensor_tensor(out=pr[:], in0=rw[:], in1=hnb[:], op=ALU.mult), reads=[rk, "hnb"], writes=[pk])
                    if hk % 5 != 4:
                        T.op("act", lambda e: e.activation(out=junka[:], in_=pr[:], func=AF.Copy, accum_out=dots[:, hk:hk + 1]),
                             reads=[pk, "dotsz"], writes=["dots%d" % hk])
                    else:
                        T.op("dve", lambda e: e.tensor_scalar(out=junkf[:], in0=pr[:], scalar1=1.0, scalar2=0.0, op0=ALU.mult, op1=ALU.add,
                                                              accum_out=dots[:, hk:hk + 1]),
                             reads=[pk, "dotsz"], writes=["dots%d" % hk])

                def v_row(t, hk):
                    sl = t % 3
                    rw, rk = next_row()
                    dg = diag[cnt["di"] % ND]
                    dgk = "diag%d" % (cnt["di"] % ND)
                    cnt["di"] += 1
                    T.dma("pool", lambda e: e.indirect_dma_start(
                        out=rw[:], out_offset=None, in_=evb_d[:, :], in_offset=bass.IndirectOffsetOnAxis(ap=eid2[:, sl, hk:hk + 1], axis=0)),
                          rk, reads=["eid%d" % sl], writes=[rk])
                    T.op("act", lambda e: e.activation(out=dg[:], in_=ident[:], func=AF.Copy, scale=wgt[:, hk:hk + 1]), reads=["wgt"], writes=[dgk])
                    for ng in range(4):
                        T.op("pe", lambda e, ng=ng: e.matmul(PSF[ng][:, 0:512], lhsT=dg[:], rhs=rw[:, ng * 512:(ng + 1) * 512],
                                                            start=(hk == 0), stop=(hk == 127)),
                             reads=[dgk, rk], writes=["psf%d" % ng])

                def make_wgt(t):
                    sl = t % 3
                    T.op("act", lambda e: e.activation(out=wgt[:], in_=dots[:], func=AF.Gelu), reads=dkeys, writes=["wgt"])
                    T.op("dve", lambda e: e.tensor_tensor(out=wgt[:], in0=wgt[:], in1=gat2[:, sl, :], op=ALU.mult), reads=["wgt", "gat%d" % sl], writes=["wgt"])

                def finish(t):
                    xs = x1s[t % 2]
                    xk = "x1s%d" % (t % 2)
                    for ng in range(4):
                        T.op("dve", lambda e, ng=ng: e.tensor_tensor(out=yacc[:, ng * 512:(ng + 1) * 512], in0=PSF[ng][:, 0:512], in1=xs[:, ng * 512:(ng + 1) * 512], op=ALU.add),
                             reads=["psf%d" % ng, xk], writes=["yacc%d" % ng])
                    c2 = NT_OWN + t
                    T.op("act", lambda e: e.activation(out=junk[:], in_=yacc[:], func=AF.Square, accum_out=ssq[:, c2:c2 + 1]),
                         reads=ykeys + ["ssqDz"], writes=["ssqE%d" % t])
                    T.op("act", lambda e: e.activation(out=rs[:, c2:c2 + 1], in_=ssq[:, c2:c2 + 1], func=AF.Sqrt, scale=1.0 / D, bias=epsc[:, 0:1]),
                         reads=["ssqE%d" % t], writes=["rsE%d" % t])
                    T.op("dve", lambda e: e.reciprocal(out=rs[:, c2:c2 + 1], in_=rs[:, c2:c2 + 1]), reads=["rsE%d" % t], writes=["rsE%d" % t])
                    T.op("dve", lambda e: e.scalar_tensor_tensor(out=yacc[:], in0=yacc[:], scalar=rs[:, c2:c2 + 1], in1=gf[:], op0=ALU.mult, op1=ALU.mult),
                         reads=ykeys + ["rsE%d" % t, "gf"], writes=ykeys)
                    T.dma("sp", lambda e: e.dma_start(out=y_d[t * 128:(t + 1) * 128, :], in_=yacc[:]), "yout", reads=ykeys)

                def zero_dots():
                    T.op("pool", lambda e: e.memset(dots[:], 0.0), writes=["dotsz"] + dkeys)

                load_x1(0)
                load_x1(1)
                play(topk_ops(0), 10 ** 6)
                play(topk_ops(1), 10 ** 6)
                make_hnb(0)
                zero_dots()
                for hk in range(128):
                    u_row(0, hk)
                make_wgt(0)
                make_hnb(1)
                for t in range(NT_OWN):
                    pend = topk_ops(t + 2) if t + 2 < NT_OWN else []
                    if t + 1 < NT_OWN:
                        zero_dots()
                    for hk in range(128):
                        v_row(t, hk)
                        if t + 1 < NT_OWN:
                            u_row(t + 1, hk)
                        play(pend, 2)
                    play(pend, 10 ** 6)
                    finish(t)
                    if t + 1 < NT_OWN:
                        make_wgt(t + 1)
                    if t + 2 < NT_OWN:
                        load_x1(t + 2)
                        make_hnb(t + 2)
                T.barrier()

    if "A" in phases:
        phase_A()
    if "B" in phases:
        phase_B()
    if "C" in phases:
        phase_C()
    if "D" in phases:
        phase_DE(do_E=("E" in phases))
    T.barrier()
    gs.close()
    return nc, T


def make_in_maps(x_prompt, x_sample, norm1_g, w_in, a_out_g, b_out_g, sink, w_out, norm2_g, w_pq,
                 sub_keys, expert_u, expert_v, normf_g):
    f = np.float32
    xs = np.concatenate([np.asarray(x_prompt, f).reshape(-1, D), np.asarray(x_sample, f).reshape(-1, D)], axis=0)
    ntot = xs.shape[0]
    assert ntot == NCORES * T_OWN
    xpad = np.zeros((ntot + 2 * HALO, D), f)
    xpad[HALO:HALO + ntot] = xs
    nprompt = np.asarray(x_prompt).shape[0] * np.asarray(x_prompt).shape[1]
    seqlen_p = np.asarray(x_prompt).shape[1]
    seg = np.full((ntot + 2 * HALO,), 6.0, f)
    ids = np.arange(ntot)
    seg[HALO:HALO + ntot] = np.where(ids < nprompt, ids // seqlen_p, nprompt // seqlen_p).astype(f)
    segq = np.stack([seg * seg, seg, np.ones_like(seg)]).astype(f)
    segk = np.stack([-SEG_B * np.ones_like(seg), 2 * SEG_B * seg, -SEG_B * seg * seg]).astype(f)
    rep = lambda v: np.ascontiguousarray(np.broadcast_to(np.asarray(v, f).reshape(1, D), (128, D)))
    gcat = np.concatenate([np.asarray(a_out_g, f).reshape(-1), np.asarray(b_out_g, f).reshape(-1)])
    gab = np.ascontiguousarray(gcat.reshape(16, 128).T)
    sk = np.asarray(sink, f).reshape(-1)
    sinkc = np.ascontiguousarray(np.broadcast_to(sk.reshape(1, 16), (128, 16)))
    shared = {
        "g1b": rep(norm1_g), "g2b": rep(norm2_g), "gfb": rep(normf_g), "gab": gab, "sinkc": sinkc,
        "w_in": np.ascontiguousarray(np.asarray(w_in, f).reshape(D, IN_W)),
        "w_out": np.ascontiguousarray(np.asarray(w_out, f).reshape(D, D)),
        "w_pq": np.ascontiguousarray(np.asarray(w_pq, f).reshape(D, D)),
        "sub_keys": np.ascontiguousarray(np.asarray(sub_keys, f).reshape(16 * 128, 128)),
        "expert_u": np.ascontiguousarray(np.asarray(expert_u, f).reshape(16384, D)),
        "expert_v": np.ascontiguousarray(np.asarray(expert_v, f).reshape(16384, D)),
    }
    in_maps = []
    for c in range(NCORES):
        lo = c * T_OWN
        m = dict(shared)
        m["x_ext"] = np.ascontiguousarray(xpad[lo:lo + T_EXT])
        m["segq"] = np.ascontiguousarray(segq[:, lo:lo + T_EXT])
        m["segk"] = np.ascontiguousarray(segk[:, lo:lo + T_EXT])
        in_maps.append(m)
    return in_maps


def kernel(x_prompt, x_sample, norm1_g, w_in, a_out_g, b_out_g, sink, w_out, norm2_g, w_pq,
           sub_keys, expert_u, expert_v, normf_g):
    in_maps = make_in_maps(x_prompt, x_sample, norm1_g, w_in, a_out_g, b_out_g, sink, w_out, norm2_g, w_pq,
                           sub_keys, expert_u, expert_v, normf_g)
    nc, _ = build_program()
    res = run_bass_kernel_spmd(nc, in_maps, core_ids=list(range(NCORES)))
    y = np.concatenate([np.asarray(r["y"], np.float32) for r in res.results], axis=0)
    xp = np.asarray(x_prompt)
    xsm = np.asarray(x_sample)
    npr = xp.shape[0] * xp.shape[1]
    return (np.ascontiguousarray(y[:npr]).reshape(xp.shape), np.ascontiguousarray(y[npr:]).reshape(xsm.shape))
```
